# Optimizing a Trainium2 kernel written in Bass

```python
import math
import jax, jax.numpy as jnp
from jax import lax
import numpy as np

D_MODEL = 1024
BATCH = 16
SEQ = 2048
DEPTH = 4

CHUNK = 64
N_META = 16
D_MIX = D_MODEL
SB_HEAD_DIM = 64
SB_WIDTH = D_MIX // 2
SB_HEADS = SB_WIDTH // SB_HEAD_DIM
CONF_WIDTH = D_MIX // 4
CONF_KERNEL = 31
SC_WIDTH = D_MIX - SB_WIDTH - CONF_WIDTH
SC_KERNEL = 3
D_IN = 3 * SB_WIDTH + 2 * CONF_WIDTH + 3 * SC_WIDTH
D_FF = 4 * D_MODEL
QBLOCK = 128
DEEPNORM_ALPHA = (2.0 * DEPTH) ** 0.25
DEEPNORM_BETA = (8.0 * DEPTH) ** -0.25
LN_EPS = 1e-5
RMS_EPS = 1e-6

kernel_name = "hymba_stickbreak_conformer_shortconv_deepnorm"


def layer_norm(x, g, b):
    xf = x.astype(jnp.float32)
    mu = jnp.mean(xf, axis=-1, keepdims=True)
    var = jnp.mean(jnp.square(xf - mu), axis=-1, keepdims=True)
    out = (xf - mu) * lax.rsqrt(var + LN_EPS) * g.astype(jnp.float32) + b.astype(jnp.float32)
    return out.astype(x.dtype)


def rms_norm(x, g):
    xf = x.astype(jnp.float32)
    out = xf * lax.rsqrt(jnp.mean(jnp.square(xf), axis=-1, keepdims=True) + RMS_EPS) * g.astype(jnp.float32)
    return out.astype(x.dtype)


def causal_depthwise_conv(x, w):
    k_width, ch = w.shape
    return lax.conv_general_dilated(
        x, w[:, None, :].astype(x.dtype), window_strides=(1,), padding=[(k_width - 1, 0)],
        dimension_numbers=("NWC", "WIO", "NWC"), feature_group_count=ch)


def stick_breaking_attention(q, k, v):
    bsz, n_heads, length, dh = q.shape
    n_blk = -(-length // QBLOCK)
    pad = ((0, 0), (0, 0), (0, n_blk * QBLOCK - length), (0, 0))
    q, k, v = jnp.pad(q, pad), jnp.pad(k, pad), jnp.pad(v, pad)
    scale = dh ** -0.5
    outs = []
    for i in range(n_blk):
        q0, kend = i * QBLOCK, (i + 1) * QBLOCK
        qb = q[:, :, q0:kend].astype(jnp.float32)
        kb = k[:, :, :kend].astype(jnp.float32)
        vb = v[:, :, :kend].astype(jnp.float32)
        z = jnp.einsum("bhtd,bhsd->bhts", qb, kb) * scale
        t_idx = q0 + jnp.arange(QBLOCK)[:, None]
        s_idx = jnp.arange(kend)[None, :]
        past = s_idx < t_idx
        log_keep = jnp.where(past, -jax.nn.softplus(z), 0.0)
        between = lax.cumsum(log_keep, axis=3, reverse=True) - log_keep
        log_a = jax.nn.log_sigmoid(z) + between
        a = jnp.where(past, jnp.exp(log_a), 0.0)
        outs.append(jnp.einsum("bhts,bhsd->bhtd", a, vb))
    o = jnp.concatenate(outs, axis=2)[:, :, :length]
    return o.astype(v.dtype)


def conformer_conv(u, w_dw, b_dw, ln_g, ln_b):
    a, gate = jnp.split(u, 2, axis=-1)
    h = a * jax.nn.sigmoid(gate)
    h = causal_depthwise_conv(h, w_dw) + b_dw.astype(h.dtype)
    h = layer_norm(h, ln_g, ln_b)
    return jax.nn.swish(h)


def short_gated_conv(u, w_dw):
    b_gate, c_gate, h = jnp.split(u, 3, axis=-1)
    return b_gate * causal_depthwise_conv(c_gate * h, w_dw)


def hybrid_mixer(h, w_in, w_conf_dw, b_conf_dw, ln_conf_g, ln_conf_b, w_short_dw, g_mix, w_out):
    bsz, length, _ = h.shape
    u = h @ w_in
    q, k, v, conf_in, sc_in = jnp.split(
        u, [SB_WIDTH, 2 * SB_WIDTH, 3 * SB_WIDTH, 3 * SB_WIDTH + 2 * CONF_WIDTH], axis=-1)

    def to_heads(t):
        return t.reshape(bsz, length, SB_HEADS, SB_HEAD_DIM).transpose(0, 2, 1, 3)

    o_sb = stick_breaking_attention(to_heads(q), to_heads(k), to_heads(v))
    o_sb = o_sb.transpose(0, 2, 1, 3).reshape(bsz, length, SB_WIDTH)
    o_conf = conformer_conv(conf_in, w_conf_dw, b_conf_dw, ln_conf_g, ln_conf_b)
    o_sc = short_gated_conv(sc_in, w_short_dw)
    g_sb, g_conf, g_sc = jnp.split(g_mix, [SB_WIDTH, SB_WIDTH + CONF_WIDTH])
    y = jnp.concatenate([rms_norm(o_sb, g_sb), rms_norm(o_conf, g_conf), rms_norm(o_sc, g_sc)], axis=-1)
    return y @ w_out


def squared_relu_mlp(h, w1, w2):
    return jnp.square(jax.nn.relu(h @ w1)) @ w2


def setup_inputs(seed: int = 0) -> dict:
    key = jax.random.key(seed)
    ks = jax.random.split(key, 20)
    f32 = jnp.float32

    def nrm(k, shape, scale):
        return jax.random.normal(k, shape, f32) * scale

    return {
        "x": nrm(ks[0], (BATCH, SEQ, D_MODEL), 1.0),
        "meta_tokens": nrm(ks[1], (N_META, D_MODEL), 1.0),
        "ln_in_g": 1.0 + nrm(ks[2], (D_MODEL,), 0.01),
        "ln_in_b": nrm(ks[3], (D_MODEL,), 0.01),
        "w_in": nrm(ks[4], (DEPTH, D_MODEL, D_IN), D_MODEL ** -0.5),
        "w_conf_dw": nrm(ks[5], (DEPTH, CONF_KERNEL, CONF_WIDTH), CONF_KERNEL ** -0.5),
        "b_conf_dw": nrm(ks[6], (DEPTH, CONF_WIDTH), 0.01),
        "ln_conf_g": 1.0 + nrm(ks[7], (DEPTH, CONF_WIDTH), 0.01),
        "ln_conf_b": nrm(ks[8], (DEPTH, CONF_WIDTH), 0.01),
        "w_short_dw": nrm(ks[9], (DEPTH, SC_KERNEL, SC_WIDTH), SC_KERNEL ** -0.5),
        "g_mix": 1.0 + nrm(ks[10], (DEPTH, D_MIX), 0.01),
        "w_out": nrm(ks[11], (DEPTH, D_MIX, D_MODEL), D_MIX ** -0.5 * DEEPNORM_BETA),
        "ln_mix_g": 1.0 + nrm(ks[12], (DEPTH, D_MODEL), 0.01),
        "ln_mix_b": nrm(ks[13], (DEPTH, D_MODEL), 0.01),
        "w_ff1": nrm(ks[14], (DEPTH, D_MODEL, D_FF), D_MODEL ** -0.5),
        "w_ff2": nrm(ks[15], (DEPTH, D_FF, D_MODEL), D_FF ** -0.5 * DEEPNORM_BETA),
        "ln_ff_g": 1.0 + nrm(ks[16], (DEPTH, D_MODEL), 0.01),
        "ln_ff_b": nrm(ks[17], (DEPTH, D_MODEL), 0.01),
    }


def reference(x, meta_tokens, ln_in_g, ln_in_b, w_in, w_conf_dw, b_conf_dw, ln_conf_g, ln_conf_b,
              w_short_dw, g_mix, w_out, ln_mix_g, ln_mix_b, w_ff1, w_ff2, ln_ff_g, ln_ff_b):
    bsz = x.shape[0]
    meta = jnp.broadcast_to(meta_tokens[None].astype(x.dtype), (bsz, N_META, D_MODEL))
    h = layer_norm(jnp.concatenate([meta, x], axis=1), ln_in_g, ln_in_b)
    for l in range(DEPTH):
        mix = hybrid_mixer(h, w_in[l], w_conf_dw[l], b_conf_dw[l], ln_conf_g[l], ln_conf_b[l],
                           w_short_dw[l], g_mix[l], w_out[l])
        h = layer_norm(DEEPNORM_ALPHA * h + mix, ln_mix_g[l], ln_mix_b[l])
        ff = squared_relu_mlp(h, w_ff1[l], w_ff2[l])
        h = layer_norm(DEEPNORM_ALPHA * h + ff, ln_ff_g[l], ln_ff_b[l])
    return h[:, N_META:]
```

```python
from contextlib import ExitStack
import numpy as np
import ml_dtypes
import concourse.bass as bass
import concourse.mybir as mybir
from concourse.bass_utils import run_bass_kernel_spmd

F32 = mybir.dt.float32
BF16 = mybir.dt.bfloat16
AF = mybir.ActivationFunctionType
ALU = mybir.AluOpType
AX = mybir.AxisListType

DEPTH = 4
D = 1024
SEQ = 2048
NMETA = 16
L = SEQ + NMETA
DIN = 2816
DFF = 4096
ALPHA = (2.0 * DEPTH) ** 0.25
LN_EPS = 1e-5
RMS_EPS = 1e-6
TT = 344
NT = 6
BS = [16] + [128] * 16
BO = [0] + [16 + 128 * i for i in range(16)]
NB = 17
ABS = [128] * 16 + [16]
ABO = [128 * i for i in range(17)]
EPOCH = 30000
NCORES = 8
NSEQ = 2

VC_LNIN_G, VC_LNIN_B = 0, 8
VC_L0 = 16
VC_PER = 114
VO_MIXG, VO_MIXB, VO_FFG, VO_FFB, VO_GMIX = 0, 8, 16, 24, 32
VO_CW = 40
VO_CB = 102
VO_CG = 104
VO_CBB = 106
VO_SW = 108
NV = VC_L0 + VC_PER * DEPTH


class Buf:
    __slots__ = ("name", "lw", "rd", "dsem", "dcount")

    def __init__(self, name):
        self.name = name
        self.lw = None
        self.rd = []
        self.dsem = None
        self.dcount = 0


class Eng:
    def __init__(self, ctx, name, h):
        self.ctx = ctx
        self.name = name
        self.h = h
        self.n = 0
        self.sems = []
        self.seen = {}

    def sem_val(self, idx):
        ep = idx // EPOCH
        while len(self.sems) <= ep:
            self.sems.append(self.ctx.nc.alloc_semaphore(f"s_{self.name}_{len(self.sems)}"))
        return self.sems[ep], idx % EPOCH + 1


class Ctx:
    def __init__(self, nc):
        self.nc = nc
        self.pe = Eng(self, "pe", nc.tensor)
        self.act = Eng(self, "act", nc.scalar)
        self.dve = Eng(self, "dve", nc.vector)
        self.pool = Eng(self, "pool", nc.gpsimd)
        self.sp = Eng(self, "sp", nc.sync)
        self.engs = [self.pe, self.act, self.dve, self.pool, self.sp]
        self.owners = []
        self.dreg = {}
        self.log = None

    def _wait(self, eng, tick):
        key, idx = tick
        if eng.seen.get(key, -1) >= idx:
            return
        if isinstance(key, Eng):
            sem, val = key.sem_val(idx)
        else:
            sem, val = key.dsem, 16 * (idx + 1)
        eng.h.wait_ge(sem, val)
        if self.log is not None:
            self.log.append((eng.name, "wait", id(sem), val))
        eng.seen[key] = idx

    @staticmethod
    def _deps(reads, writes):
        deps = []
        for b in reads:
            if b.lw is not None:
                deps.append(b.lw)
        for b in writes:
            if b.lw is not None:
                deps.append(b.lw)
            deps.extend(b.rd)
        return deps

    @staticmethod
    def _record(tick, reads, writes):
        for b in reads:
            b.rd.append(tick)
        for b in writes:
            b.lw = tick
            b.rd = []

    def op(self, eng, reads, writes, emit):
        for t in self._deps(reads, writes):
            if t[0] is eng and eng is self.pe:
                continue
            self._wait(eng, t)
        inst = emit()
        idx = eng.n
        eng.n += 1
        sem, _ = eng.sem_val(idx)
        inst.then_inc(sem, 1)
        if self.log is not None:
            self.log.append((eng.name, "inc", id(sem), 1))
        self._record((eng, idx), reads, writes)

    def dma(self, eng, owner, reads, writes, out, in_, **kw):
        if owner.dsem is None:
            if owner.name in self.dreg:
                owner.dsem, owner.dcount = self.dreg[owner.name]
            else:
                owner.dsem = self.nc.alloc_semaphore(f"d_{owner.name}")
            self.owners = [o for o in self.owners if o.name != owner.name] + [owner]
        deps = self._deps(reads, writes)
        if owner.dcount > 0:
            deps.append((owner, owner.dcount - 1))
        for t in deps:
            self._wait(eng, t)
        inst = eng.h.dma_start(out=out, in_=in_, **kw)
        idx = owner.dcount
        owner.dcount += 1
        inst.then_inc(owner.dsem, 16)
        if self.log is not None:
            self.log.append((eng.name, "dmainc", id(owner.dsem), 16))
        self.dreg[owner.name] = (owner.dsem, owner.dcount)
        self._record((owner, idx), reads, writes)

    def barrier(self):
        ticks = [(e, e.n - 1) for e in self.engs if e.n > 0]
        ticks += [(o, o.dcount - 1) for o in self.owners if o.dcount > 0]
        for e in self.engs:
            for t in ticks:
                self._wait(e, t)


def build(nl=DEPTH, nseq=NSEQ, log=None):
    nc = bass.Bass("TRN2", target_bir_lowering=False)

    def din(name, shape, dt=F32):
        return nc.dram_tensor(name, list(shape), dt, kind="ExternalInput").ap()

    x_d = din("x", [nseq, SEQ, D])
    meta_d = din("meta", [NMETA, D])
    w_in_d = din("w_in", [DEPTH, D, DIN])
    w_out_d = din("w_out", [DEPTH, D, D])
    w_ff1_d = din("w_ff1", [DEPTH, D, DFF])
    w_ff2_d = din("w_ff2", [DEPTH, DFF, D])
    vecs_d = din("vecs", [128, NV])
    cf32_d = din("cf32", [128, 256])
    cb16_d = din("cb16", [128, 640], BF16)
    out_d = nc.dram_tensor("out", [nseq, SEQ, D], F32, kind="ExternalOutput").ap()

    c = Ctx(nc)
    c.log = log
    uid = [0]

    def nm(s):
        uid[0] += 1
        return f"{s}_{uid[0]}"

    cur_scope = [None]

    def sc(name):
        if cur_scope[0] is not None:
            nc.leave_named_scope(cur_scope[0][0], cur_scope[0][1], False)
            cur_scope[0] = None
        if name is not None:
            sid, _ = nc.enter_named_scope(name, False)
            cur_scope[0] = (name, sid)

    H32 = nc.alloc_sbuf_tensor("H32", [128, 8, L], F32)
    HB = nc.alloc_sbuf_tensor("HB", [128, 8, L], BF16)
    VEC = nc.alloc_sbuf_tensor("VEC", [128, NV], F32)
    CF = nc.alloc_sbuf_tensor("CF", [128, 256], F32)
    CB = nc.alloc_sbuf_tensor("CB", [128, 640], BF16)
    IDENT = CF[:, 0:128]
    MASK = CF[:, 128:256]
    UI = CB[:, 0:128]
    LX = CB[:, 128:256]
    ON1024 = CB[:, 256:384]
    ON512 = CB[:, 384:512]
    ON256 = CB[:, 512:640]
    PS = [nc.alloc_psum_tensor(f"ps{i}", [128, 512], F32) for i in range(8)]

    def V_(col):
        return VEC[:, col:col + 1]

    def act(reads, writes, out, in_, func, scale=None, bias=None):
        kw = {}
        if scale is not None:
            kw["scale"] = scale
        if bias is not None:
            kw["bias"] = bias
        c.op(c.act, reads, writes, lambda: nc.scalar.activation(out=out, in_=in_, func=func, **kw))

    def dve_tt(reads, writes, out, in0, in1, op):
        c.op(c.dve, reads, writes, lambda: nc.vector.tensor_tensor(out=out, in0=in0, in1=in1, op=op))

    def dve_ts(reads, writes, out, in0, s1, s2, op0, op1=None):
        if op1 is None:
            c.op(c.dve, reads, writes,
                 lambda: nc.vector.tensor_scalar(out=out, in0=in0, scalar1=s1, scalar2=None, op0=op0))
        else:
            c.op(c.dve, reads, writes,
                 lambda: nc.vector.tensor_scalar(out=out, in0=in0, scalar1=s1, scalar2=s2, op0=op0, op1=op1))

    def dve_stt(reads, writes, out, in0, scalar, in1, op0, op1):
        c.op(c.dve, reads, writes,
             lambda: nc.vector.scalar_tensor_tensor(out=out, in0=in0, scalar=scalar, in1=in1, op0=op0, op1=op1))

    def dve_copy(reads, writes, out, in_):
        c.op(c.dve, reads, writes, lambda: nc.vector.tensor_copy(out=out, in_=in_))

    def dve_recip(reads, writes, out, in_):
        c.op(c.dve, reads, writes, lambda: nc.vector.reciprocal(out=out, in_=in_))

    def mm_group(reads, writes, mms):
        def emit():
            last = None
            for (o, l_, r_, st, sp_) in mms:
                last = nc.tensor.matmul(o, lhsT=l_, rhs=r_, start=st, stop=sp_, skip_group_check=True)
            return last
        c.op(c.pe, reads, writes, emit)

    def rstd_from(reads_b, out_b, out_ap, in_ap, eps):
        act(reads_b, [out_b], out_ap, in_ap, AF.Ln, bias=eps)
        act([out_b], [out_b], out_ap, out_ap, AF.Exp, scale=-0.5)

    with nc.Block():
        bV, bCF, bCB = Buf("VEC"), Buf("CF"), Buf("CB")
        c.dma(c.sp, bV, [], [bV], out=VEC[:], in_=vecs_d[:, :])
        c.dma(c.sp, bCF, [], [bCF], out=CF[:], in_=cf32_d[:, :])
        c.dma(c.sp, bCB, [], [bCB], out=CB[:], in_=cb16_d[:, :])
        c.barrier()

        for s in range(nseq):
            sc(f"p0_s{s}")
            with ExitStack() as st:
                NP0 = 3
                XT = [st.enter_context(nc.sbuf_tensor(nm("XT"), [128, D], F32)) for _ in range(NP0)]
                SQT = [st.enter_context(nc.sbuf_tensor(nm("SQT"), [128, D], F32)) for _ in range(NP0)]
                STAT = [st.enter_context(nc.sbuf_tensor(nm("STAT"), [128, 4], F32)) for _ in range(NP0)]
                bXT = [Buf(f"XT{i}") for i in range(NP0)]
                bSQ = [Buf(f"SQT{i}") for i in range(NP0)]
                bST = [Buf(f"ST{i}") for i in range(NP0)]
                bP = [Buf(f"P{i}") for i in range(8)]
                bHblk = [Buf(f"H32_{j}") for j in range(NB)]
                bHBblk = [Buf(f"HB_{j}") for j in range(NB)]

                def p0_block(blk):
                    bs, k0 = BS[blk], BO[blk]
                    i = blk % NP0
                    xt, stt, sqt = XT[i], STAT[i], SQT[i]
                    src = meta_d[:, :] if blk == 0 else x_d[s, (blk - 1) * 128: blk * 128, :]
                    c.dma(c.sp, bXT[i], [], [bXT[i]], out=xt[:bs, :], in_=src)
                    yield
                    c.op(c.dve, [bXT[i]], [bST[i]],
                         lambda: nc.vector.reduce_sum(out=stt[:bs, 0:1], in_=xt[:bs, :], axis=AX.X))
                    yield
                    dve_ts([bST[i]], [bST[i]], stt[:bs, 1:2], stt[:bs, 0:1], -1.0 / D, None, ALU.mult)
                    yield
                    dve_ts([bXT[i], bST[i]], [bXT[i]], xt[:bs, :], xt[:bs, :], stt[:bs, 1:2], None, ALU.add)
                    yield
                    act([bXT[i]], [bSQ[i]], sqt[:bs, :], xt[:bs, :], AF.Square)
                    yield
                    c.op(c.dve, [bSQ[i]], [bST[i]],
                         lambda: nc.vector.reduce_sum(out=stt[:bs, 2:3], in_=sqt[:bs, :], axis=AX.X))
                    yield
                    act([bST[i]], [bST[i]], stt[:bs, 3:4], stt[:bs, 2:3], AF.Ln, scale=1.0 / D, bias=LN_EPS)
                    act([bST[i]], [bST[i]], stt[:bs, 3:4], stt[:bs, 3:4], AF.Exp, scale=-0.5)
                    yield
                    dve_ts([bXT[i], bST[i]], [bXT[i]], xt[:bs, :], xt[:bs, :], stt[:bs, 3:4], None, ALU.mult)
                    yield
                    for half in range(2):
                        pb = 2 * i + half

                        def emit_t(half=half, pb=pb):
                            last = None
                            for j in range(4):
                                ch = half * 4 + j
                                last = nc.tensor.transpose(PS[pb][:, j * 128: j * 128 + bs],
                                                           xt[:bs, ch * 128:(ch + 1) * 128], IDENT[:bs, :bs])
                            return last
                        c.op(c.pe, [bXT[i]], [bP[pb]], emit_t)
                        yield
                        for j in range(4):
                            ch = half * 4 + j
                            act([bP[pb]], [bHblk[blk]], H32[:, ch, k0:k0 + bs], PS[pb][:, j * 128: j * 128 + bs],
                                AF.Identity, scale=V_(VC_LNIN_G + ch), bias=V_(VC_LNIN_B + ch))
                        yield
                    dve_copy([bHblk[blk]], [bHBblk[blk]], HB[:, :, k0:k0 + bs], H32[:, :, k0:k0 + bs])
                    yield

                active, nxt = [], 0
                while active or nxt < NB:
                    if nxt < NB and len(active) < NP0:
                        active.append(p0_block(nxt))
                        nxt += 1
                    for gen in list(active):
                        try:
                            next(gen)
                        except StopIteration:
                            active.remove(gen)
                c.barrier()

            for l in range(nl):
                vb = VC_L0 + VC_PER * l
                with ExitStack() as stA:
                    YC = stA.enter_context(nc.sbuf_tensor(nm("YC"), [128, 4, L], BF16))
                    for sub in range(2):
                        sc(f"{"AC" if sub == 0 else "A1"}_s{s}_l{l}")
                        with ExitStack() as st:
                            SL = [st.enter_context(nc.sbuf_tensor(nm("SL"), [128, 4096], BF16)) for _ in range(2)]
                            bSL = [Buf("SL0"), Buf("SL1"), Buf("SL2")]
                            GLT = [st.enter_context(nc.sbuf_tensor(nm("GLT"), [128, 2, 30 + TT], BF16)) for _ in range(2)]
                            if sub == 0:
                                DM = st.enter_context(nc.sbuf_tensor(nm("DM"), [128, 62, 128], BF16))
                            bDM, bDM2 = Buf("DM"), Buf("DM2")
                            CHT = [st.enter_context(nc.sbuf_tensor(nm("CHT"), [128, 2, 2 + TT], F32)) for _ in range(2)]
                            npar = 2 if sub == 0 else 0
                            T1 = [st.enter_context(nc.sbuf_tensor(nm("T1"), [128, 2, TT], F32)) for _ in range(npar)]
                            T2 = [st.enter_context(nc.sbuf_tensor(nm("T2"), [128, 2, TT], F32)) for _ in range(npar)]
                            T3 = [st.enter_context(nc.sbuf_tensor(nm("T3"), [128, 2, TT], F32)) for _ in range(npar)]
                            TB = [st.enter_context(nc.sbuf_tensor(nm("TB"), [128, 2, TT], BF16)) for _ in range(npar)]
                            RS = [st.enter_context(nc.sbuf_tensor(nm("RS"), [128, TT], F32)) for _ in range(npar)]
                            bGL = [Buf("GL0"), Buf("GL1")]
                            bCH = [Buf("CH0"), Buf("CH1")]
                            bT1, bT2, bT3, bTB, bRS = ([Buf(n + "a"), Buf(n + "b")] for n in ("T1", "T2", "T3", "TB", "RS"))

                            def run_interleaved(make_gen):
                                active, nxt, can_start = [], 0, True
                                while active or nxt < NT:
                                    if can_start and nxt < NT and len(active) < 2:
                                        active.append(make_gen(nxt))
                                        nxt += 1
                                        can_start = False
                                    for gen in list(active):
                                        try:
                                            if next(gen) == "halo":
                                                can_start = True
                                        except StopIteration:
                                            active.remove(gen)
                            bP = [Buf(f"P{i}") for i in range(8)]
                            bHB = Buf("HB")
                            bQ, bK, bVt, bYC = Buf("QT"), Buf("KT"), Buf("VT"), Buf("YC")
                            prr = [0]

                            def nextps():
                                i = prr[0] % (4 if sub == 0 else 6)
                                prr[0] += 1
                                return i

                            def slabview(i, w):
                                return SL[i][:, 0:8 * w].rearrange("p (k n) -> p k n", k=8)

                            def load_slab(i, w, parts):
                                sv = slabview(i, w)
                                for (off, c0, n) in parts:
                                    c.dma(c.pool, bSL[i], [], [bSL[i]], out=sv[:, :, off:off + n],
                                          in_=w_in_d[l, :, c0:c0 + n].rearrange("(k p) n -> p k n", p=128))

                            units = [("conf",), ("sc",)] if sub == 0 else [("qk", p) for p in range(4)] + [("v",)]

                            def unit_loads(u, i):
                                if u[0] == "qk":
                                    p = u[1]
                                    load_slab(i, 256, [(0, 128 * p, 128), (128, 512 + 128 * p, 128)])
                                elif u[0] == "v":
                                    load_slab(i, 512, [(0, 1024, 512)])
                                elif u[0] == "conf":
                                    load_slab(i, 512, [(0, 1536, 512)])

                            def dense_mm(pi, sv, m0, t0, n):
                                mm_group([bHB, bSLcur[0]], [bP[pi]],
                                         [(PS[pi][:, :n], sv[:, k, m0:m0 + 128], HB[:, k, t0:t0 + n], k == 0, k == 7)
                                          for k in range(8)])

                            bSLcur = [None]
                            unit_loads(units[0], 0)
                            for ui, u in enumerate(units):
                                i = ui % 2
                                if u[0] != "sc":
                                    if ui + 1 < len(units) and units[ui + 1][0] != "sc":
                                        unit_loads(units[ui + 1], (ui + 1) % 2)
                                    bSLcur[0] = bSL[i]
                                if u[0] == "qk":
                                    p = u[1]
                                    sv = slabview(i, 256)
                                    for t in range(NT):
                                        t0 = t * TT
                                        pq = nextps()
                                        dense_mm(pq, sv, 0, t0, TT)
                                        act([bP[pq]], [bQ], QT[:, p, t0:t0 + TT], PS[pq][:, :TT], AF.Identity, scale=0.125)
                                        pk = nextps()
                                        dense_mm(pk, sv, 128, t0, TT)
                                        dve_copy([bP[pk]], [bK], KT[:, p, t0:t0 + TT], PS[pk][:, :TT])
                                elif u[0] == "v":
                                    sv = slabview(i, 512)
                                    for blk in range(NB):
                                        bs, k0 = ABS[blk], ABO[blk]
                                        pv = nextps()
                                        mm_group([bHB, bSL[i]], [bP[pv]],
                                                 [(PS[pv][:bs, :], HB[:, k, k0:k0 + bs], sv[:, k, :], k == 0, k == 7)
                                                  for k in range(8)])
                                        if blk % 2 == 0:
                                            act([bP[pv]], [bVt], VT[:bs, blk, :], PS[pv][:bs, :], AF.Copy)
                                        else:
                                            dve_copy([bP[pv]], [bVt], VT[:bs, blk, :], PS[pv][:bs, :])
                                elif u[0] == "conf":
                                    sv = slabview(i, 512)
                                    for cc in range(2):
                                        for k in range(31):
                                            if k % 2 == 0:
                                                dve_ts([], [bDM], DM[:, cc * 31 + k, :], IDENT, V_(vb + VO_CW + cc * 31 + k),
                                                       None, ALU.mult)
                                            else:
                                                act([], [bDM2], DM[:, cc * 31 + k, :], IDENT, AF.Identity,
                                                    scale=V_(vb + VO_CW + cc * 31 + k))

                                    def conf_tile(t):
                                        t0 = t * TT
                                        g = t % 2
                                        gl = GLT[g]
                                        t1, t2, t3, tb, rs = T1[g], T2[g], T3[g], TB[g], RS[g]
                                        b1, b2, b3, bb, br = bT1[g], bT2[g], bT3[g], bTB[g], bRS[g]
                                        pm, pv_, pr = (6, 7, 6) if g == 0 else (4, 5, 4)
                                        if t == 0:
                                            c.op(c.dve, [], [bGL[g]], lambda: nc.vector.memset(gl[:, :, 0:30], 0.0))
                                        for cc in range(2):
                                            pg = nextps()
                                            dense_mm(pg, sv, 256 + 128 * cc, t0, TT)
                                            pa = nextps()
                                            dense_mm(pa, sv, 128 * cc, t0, TT)
                                            yield
                                            act([bP[pg]], [b1], t1[:, cc, :], PS[pg][:, :TT], AF.Exp, scale=-1.0)
                                            act([b1], [b1], t1[:, cc, :], t1[:, cc, :], AF.Ln, bias=1.0)
                                            act([b1], [b1], t1[:, cc, :], t1[:, cc, :], AF.Exp, scale=-1.0)
                                            yield
                                            dve_tt([b1, bP[pa]], [bGL[g]], gl[:, cc, 30:30 + TT], PS[pa][:, :TT],
                                                   t1[:, cc, :], ALU.mult)
                                            yield
                                        if t + 1 < NT:
                                            gn = GLT[(t + 1) % 2]
                                            dve_copy([bGL[g]], [bGL[(t + 1) % 2]], gn[:, :, 0:30], gl[:, :, TT:TT + 30])
                                        yield "halo"
                                        for cc in range(2):
                                            pc = nextps()
                                            mm_group([bGL[g], bDM, bDM2], [bP[pc]],
                                                     [(PS[pc][:, :TT], DM[:, cc * 31 + k, :], gl[:, cc, k:k + TT], k == 0, k == 30)
                                                      for k in range(31)])
                                            yield
                                            act([bP[pc]], [b2], t2[:, cc, :], PS[pc][:, :TT], AF.Identity,
                                                bias=V_(vb + VO_CB + cc))
                                            yield
                                        act([b2], [bb], tb[:], t2[:], AF.Copy)
                                        yield
                                        mm_group([bb], [bP[pm]], [(PS[pm][:, :TT], ON256, tb[:, cc, :], cc == 0, cc == 1)
                                                                  for cc in range(2)])
                                        yield
                                        for cc in range(2):
                                            dve_tt([b2, bP[pm]], [b2], t2[:, cc, :], t2[:, cc, :], PS[pm][:, :TT], ALU.subtract)
                                        yield
                                        act([b2], [bb], tb[:], t2[:], AF.Square)
                                        yield
                                        mm_group([bb], [bP[pv_]], [(PS[pv_][:, :TT], ON256, tb[:, cc, :], cc == 0, cc == 1)
                                                                   for cc in range(2)])
                                        yield
                                        rstd_from([bP[pv_]], br, rs[:], PS[pv_][:, :TT], LN_EPS)
                                        yield
                                        for cc in range(2):
                                            dve_tt([b2, br], [b2], t2[:, cc, :], t2[:, cc, :], rs[:], ALU.mult)
                                            yield
                                            act([b2], [b2], t2[:, cc, :], t2[:, cc, :], AF.Identity,
                                                scale=V_(vb + VO_CG + cc), bias=V_(vb + VO_CBB + cc))
                                            yield
                                        act([b2], [b3], t3[:], t2[:], AF.Exp, scale=-1.0)
                                        act([b3], [b3], t3[:], t3[:], AF.Ln, bias=1.0)
                                        act([b3], [b3], t3[:], t3[:], AF.Exp, scale=-1.0)
                                        yield
                                        dve_tt([b2, b3], [b2], t2[:], t2[:], t3[:], ALU.mult)
                                        yield
                                        act([b2], [bb], tb[:], t2[:], AF.Square)
                                        yield
                                        mm_group([bb], [bP[pr]], [(PS[pr][:, :TT], ON256, tb[:, cc, :], cc == 0, cc == 1)
                                                                  for cc in range(2)])
                                        yield
                                        rstd_from([bP[pr]], br, rs[:], PS[pr][:, :TT], RMS_EPS)
                                        yield
                                        for cc in range(2):
                                            dve_stt([b2, br], [bYC], YC[:, cc, t0:t0 + TT], t2[:, cc, :],
                                                    V_(vb + VO_GMIX + 4 + cc), rs[:], ALU.mult, ALU.mult)
                                        yield

                                    run_interleaved(conf_tile)
                                elif u[0] == "sc":
                                    load_slab(0, 512, [(0, 2048, 512)])
                                    load_slab(1, 256, [(0, 2560, 256)])
                                    svA = slabview(0, 512)
                                    svB = slabview(1, 256)

                                    def sc_tile(t):
                                        t0 = t * TT
                                        g = t % 2
                                        ch = CHT[g]
                                        t1, t2, t3, tb, rs = T1[g], T2[g], T3[g], TB[g], RS[g]
                                        b1, b2, b3, bb, br = bT1[g], bT2[g], bT3[g], bTB[g], bRS[g]
                                        pr = 7 if g == 0 else 5
                                        if t == 0:
                                            c.op(c.dve, [], [bCH[g]], lambda: nc.vector.memset(ch[:, :, 0:2], 0.0))
                                        for cc in range(2):
                                            pB = nextps()
                                            bSLcur[0] = bSL[0]
                                            dense_mm(pB, svA, 128 * cc, t0, TT)
                                            pC = nextps()
                                            dense_mm(pC, svA, 256 + 128 * cc, t0, TT)
                                            pH = nextps()
                                            bSLcur[0] = bSL[1]
                                            dense_mm(pH, svB, 128 * cc, t0, TT)
                                            yield
                                            act([bP[pC]], [b1], t1[:, cc, :], PS[pC][:, :TT], AF.Copy)
                                            act([bP[pB]], [b3], t3[:, cc, :], PS[pB][:, :TT], AF.Copy)
                                            yield
                                            dve_tt([b1, bP[pH]], [bCH[g]], ch[:, cc, 2:2 + TT], PS[pH][:, :TT], t1[:, cc, :],
                                                   ALU.mult)
                                            yield
                                        if t + 1 < NT:
                                            chn = CHT[(t + 1) % 2]
                                            dve_copy([bCH[g]], [bCH[(t + 1) % 2]], chn[:, :, 0:2], ch[:, :, TT:TT + 2])
                                        yield "halo"
                                        for cc in range(2):
                                            wc = vb + VO_SW + cc * 3
                                            dve_ts([bCH[g]], [b2], t2[:, cc, :], ch[:, cc, 0:TT], V_(wc), None, ALU.mult)
                                            yield
                                            for k in range(1, 3):
                                                dve_stt([bCH[g], b2], [b2], t2[:, cc, :], ch[:, cc, k:k + TT], V_(wc + k),
                                                        t2[:, cc, :], ALU.mult, ALU.add)
                                                yield
                                            dve_tt([b2, b3], [b2], t2[:, cc, :], t2[:, cc, :], t3[:, cc, :], ALU.mult)
                                            yield
                                        act([b2], [bb], tb[:], t2[:], AF.Square)
                                        yield
                                        mm_group([bb], [bP[pr]], [(PS[pr][:, :TT], ON256, tb[:, cc, :], cc == 0, cc == 1)
                                                                  for cc in range(2)])
                                        yield
                                        rstd_from([bP[pr]], br, rs[:], PS[pr][:, :TT], RMS_EPS)
                                        yield
                                        for cc in range(2):
                                            dve_stt([b2, br], [bYC], YC[:, 2 + cc, t0:t0 + TT], t2[:, cc, :],
                                                    V_(vb + VO_GMIX + 6 + cc), rs[:], ALU.mult, ALU.mult)
                                        yield

                                    run_interleaved(sc_tile)
                            c.barrier()
                        if sub == 0:
                            QT = stA.enter_context(nc.sbuf_tensor(nm("QT"), [128, 4, L], BF16))
                            KT = stA.enter_context(nc.sbuf_tensor(nm("KT"), [128, 4, L], BF16))
                            VT = stA.enter_context(nc.sbuf_tensor(nm("VT"), [128, NB, 512], BF16))

                    sc(f"A2_s{s}_l{l}")
                    with ExitStack() as st:
                        E = [[st.enter_context(nc.sbuf_tensor(nm("E"), [128, 512], F32)) for _ in range(2)] for _ in range(2)]
                        SPT = [[st.enter_context(nc.sbuf_tensor(nm("SP"), [128, 512], BF16)) for _ in range(2)] for _ in range(2)]
                        WT = [st.enter_context(nc.sbuf_tensor(nm("WT"), [128, 512], F32)) for _ in range(2)]
                        AT = [[st.enter_context(nc.sbuf_tensor(nm("AT"), [128, 512], BF16)) for _ in range(2)] for _ in range(2)]
                        OT = st.enter_context(nc.sbuf_tensor(nm("OT"), [128, 4, 512], F32))
                        SQ = st.enter_context(nc.sbuf_tensor(nm("SQ"), [128, 4, 512], BF16))
                        RS = st.enter_context(nc.sbuf_tensor(nm("RSa"), [128, 512], F32))
                        bE = [[Buf("E"), Buf("E")] for _ in range(2)]
                        bSP = [[Buf("SP"), Buf("SP")] for _ in range(2)]
                        bW = [Buf("W"), Buf("W")]
                        bA = [[Buf("A"), Buf("A")] for _ in range(2)]
                        bOT, bSQ, bRS = Buf("OT"), Buf("SQ"), Buf("RS")
                        bZ = [[Buf("Z"), Buf("Z")] for _ in range(2)]
                        bACC = [Buf("ACC"), Buf("ACC")]
                        bPO = [Buf("PO"), Buf("PO")]
                        bQK, bVt, bY = Buf("QK"), Buf("VT"), Buf("Y")
                        Z = [[PS[0], PS[1]], [PS[2], PS[3]]]
                        ACC = [PS[4], PS[5]]
                        PO = [PS[6], PS[7]]
                        cnt = [0, 0]
                        sweep_no = [0]

                        groups = [(0, NMETA)] + [(NMETA + 512 * g, 512) for g in range(4)]
                        for gi, (q0, NQ) in enumerate(groups):
                            items = []
                            for p in range(4):
                                if gi == 0:
                                    steps = [(0, 0, 16, (0, 16))]
                                else:
                                    base = 4 * (gi - 1)
                                    steps = []
                                    for j in (4, 3, 2, 1, 0):
                                        lo_j = max(0, 128 * j - 16)
                                        mk = (16, 112) if j == 0 else (0, min(128, NQ - lo_j))
                                        steps.append((base + j, lo_j, ABS[base + j], mk))
                                    steps += [(blk, 0, 128, None) for blk in range(base - 1, -1, -1)]
                                po_i = sweep_no[0] % 2
                                sweep_no[0] += 1
                                for si, (blk, lo, sbs, mk) in enumerate(steps):
                                    for h in range(2):
                                        prev = steps[si - 1] if si > 0 else None
                                        items.append(dict(p=p, h=h, blk=blk, lo=lo, bs=sbs, mask=mk, first=(si == 0),
                                                          last=(si == len(steps) - 1), prev=prev, b=None, po=po_i))
                            for it in items:
                                it["b"] = cnt[it["h"]] % 2
                                cnt[it["h"]] += 1

                            def Zs(it):
                                h, b, p, blk, lo = it["h"], it["b"], it["p"], it["blk"], it["lo"]
                                bs, k0 = it["bs"], ABO[blk]
                                mm_group([bQK], [bZ[h][b]],
                                         [(Z[h][b][:bs, lo:NQ], KT[64 * h:64 * h + 64, p, k0:k0 + bs],
                                           QT[64 * h:64 * h + 64, p, q0 + lo:q0 + NQ], True, True)])

                            def EXPs(it):
                                h, b, blk, lo = it["h"], it["b"], it["blk"], it["lo"]
                                bs = it["bs"]
                                act([bZ[h][b]], [bE[h][b]], E[h][b][:bs, lo:NQ], Z[h][b][:bs, lo:NQ], AF.Exp)
                                if it["mask"] is not None:
                                    m0, w = it["mask"]
                                    dve_tt([bE[h][b]], [bE[h][b]], E[h][b][:bs, lo:lo + w], E[h][b][:bs, lo:lo + w],
                                           MASK[:bs, m0:m0 + w], ALU.mult)

                            def LNs(it):
                                h, b, blk, lo = it["h"], it["b"], it["blk"], it["lo"]
                                bs = it["bs"]
                                act([bE[h][b]], [bSP[h][b]], SPT[h][b][:bs, lo:NQ], E[h][b][:bs, lo:NQ], AF.Ln, bias=1.0)

                            def CUMs(it):
                                h, b, blk, lo = it["h"], it["b"], it["blk"], it["lo"]
                                bs = it["bs"]
                                mms = []
                                reads = [bSP[h][b]]
                                if it["prev"] is not None:
                                    pblk, plo, pbs, _ = it["prev"]
                                    mms.append((ACC[h][:bs, plo:NQ], LX[:pbs, :bs], SPT[h][1 - b][:pbs, plo:NQ], False, False))
                                    reads.append(bSP[h][1 - b])
                                if it["first"]:
                                    mms.append((ACC[h][:, lo:NQ], UI[:bs, :128], SPT[h][b][:bs, lo:NQ], True, True))
                                else:
                                    mms.append((ACC[h][:bs, lo:NQ], UI[:bs, :bs], SPT[h][b][:bs, lo:NQ], False, True))
                                mm_group(reads, [bACC[h]], mms)

                            def EXPNEGs(it):
                                h, b, blk, lo = it["h"], it["b"], it["blk"], it["lo"]
                                bs = it["bs"]
                                act([bACC[h]], [bW[h]], WT[h][:bs, lo:NQ], ACC[h][:bs, lo:NQ], AF.Exp, scale=-1.0)
                                dve_tt([bE[h][b], bW[h]], [bA[h][b]], AT[h][b][:bs, lo:NQ], E[h][b][:bs, lo:NQ],
                                       WT[h][:bs, lo:NQ], ALU.mult)

                            def AVs(it):
                                h, b, blk, lo = it["h"], it["b"], it["blk"], it["lo"]
                                bs = it["bs"]
                                hh = 2 * it["p"] + h
                                po = it["po"]
                                mm_group([bA[h][b], bVt], [bPO[po]],
                                         [(PO[po][64 * h:64 * h + 64, lo:NQ], VT[:bs, blk, 64 * hh:64 * hh + 64],
                                           AT[h][b][:bs, lo:NQ], it["first"], it["last"])])
                                if it["last"] and h == 1:
                                    act([bPO[po]], [bOT], OT[:, it["p"], :NQ], PO[po][:, :NQ], AF.Copy)

                            NP = len(items) // 2

                            def both(fn, r):
                                if 0 <= r < NP:
                                    fn(items[2 * r])
                                    fn(items[2 * r + 1])

                            for r in range(-2, NP):
                                both(Zs, r + 2)
                                both(EXPs, r + 1)
                                both(EXPNEGs, r)
                                both(LNs, r + 1)
                                both(CUMs, r + 1)
                                both(AVs, r)
                            pr = 0
                            act([bOT], [bSQ], SQ[:, :, :NQ], OT[:, :, :NQ], AF.Square)
                            mm_group([bSQ], [bZ[0][0]], [(PS[pr][:, :NQ], ON512, SQ[:, p, :NQ], p == 0, p == 3)
                                                         for p in range(4)])
                            rstd_from([bZ[0][0]], bRS, RS[:, :NQ], PS[pr][:, :NQ], RMS_EPS)
                            for p in range(4):
                                dve_stt([bOT, bRS], [bY], HB[:, p, q0:q0 + NQ], OT[:, p, :NQ], V_(vb + VO_GMIX + p),
                                        RS[:, :NQ], ALU.mult, ALU.mult)
                        c.barrier()

                    sc(f"B_s{s}_l{l}")
                    stB = ExitStack()
                    SLb = [stB.enter_context(nc.sbuf_tensor(nm("SLb"), [128, 8, 256], BF16)) for _ in range(2)]
                    bSLb = [Buf("SLb0"), Buf("SLb1")]
                    bHrow = [Buf(f"H{t}") for t in range(NT)]
                    bYall = Buf("Yall")
                    bP = [Buf(f"P{i}") for i in range(8)]

                    def load_wout(i, mp):
                        c.dma(c.pool, bSLb[i], [], [bSLb[i]], out=SLb[i][:],
                              in_=w_out_d[l, :, 256 * mp:256 * mp + 256].rearrange("(k p) n -> p k n", p=128))

                    load_wout(0, 0)
                    pi = 0
                    for mp in range(4):
                        i = mp % 2
                        if mp + 1 < 4:
                            load_wout((mp + 1) % 2, mp + 1)
                        for mm_ in range(2):
                            m = 2 * mp + mm_
                            for t in range(NT):
                                t0 = t * TT
                                pp = pi % 4
                                pi += 1
                                mm_group([bYall, bSLb[i]], [bP[pp]],
                                         [(PS[pp][:, :TT], SLb[i][:, k, 128 * mm_:128 * mm_ + 128],
                                           (HB[:, k, t0:t0 + TT] if k < 4 else YC[:, k - 4, t0:t0 + TT]), k == 0, k == 7)
                                          for k in range(8)])
                                dve_stt([bP[pp], bHrow[t]], [bHrow[t]], H32[:, m, t0:t0 + TT], H32[:, m, t0:t0 + TT],
                                        ALPHA, PS[pp][:, :TT], ALU.mult, ALU.add)
                    c.barrier()
                    stB.close()
                with ExitStack() as st:
                    HB2 = st.enter_context(nc.sbuf_tensor(nm("HB2"), [128, 8, L], BF16))
                    SBF = [st.enter_context(nc.sbuf_tensor(nm("SBF"), [128, 8, TT], BF16)) for _ in range(3)]
                    SL1 = [st.enter_context(nc.sbuf_tensor(nm("SL1"), [128, 8, 512], BF16)) for _ in range(2)]
                    SL2 = [st.enter_context(nc.sbuf_tensor(nm("SL2"), [128, 4, 1024], BF16)) for _ in range(2)]
                    HID = [st.enter_context(nc.sbuf_tensor(nm("HID"), [128, 4, TT], BF16)) for _ in range(2)]
                    RL = [st.enter_context(nc.sbuf_tensor(nm("RL"), [128, TT], F32)) for _ in range(4)]
                    bSBF = [Buf("SBF0"), Buf("SBF1"), Buf("SBF2")]
                    bSL1 = [Buf("SL1a"), Buf("SL1b")]
                    bSL2 = [Buf("SL2a"), Buf("SL2b")]
                    bHID = [Buf("HID0"), Buf("HID1")]
                    bRL = [Buf(f"RL{j}") for j in range(4)]
                    bP = [Buf(f"P{i}") for i in range(8)]
                    bHrow = [Buf(f"H{t}") for t in range(NT)]
                    bHB2 = [Buf(f"HB2_{t}") for t in range(NT)]
                    bHBo = [Buf(f"HBo_{t}") for t in range(NT)]

                    def layer_norm_tiles(gcol, bcol, dst, bdst):
                        for w0 in range(0, NT, 3):
                            tl = list(range(w0, w0 + 3))
                            for i, t in enumerate(tl):
                                act([bHrow[t]], [bSBF[i]], SBF[i][:], H32[:, :, t * TT:(t + 1) * TT], AF.Copy)
                            for i, t in enumerate(tl):
                                pm = 2 * i
                                mm_group([bSBF[i]], [bP[pm]], [(PS[pm][:, :TT], ON1024, SBF[i][:, k, :], k == 0, k == 7)
                                                               for k in range(8)])
                            for k in range(8):
                                for i, t in enumerate(tl):
                                    t0 = t * TT
                                    dve_tt([bHrow[t], bP[2 * i]], [bHrow[t]], H32[:, k, t0:t0 + TT], H32[:, k, t0:t0 + TT],
                                           PS[2 * i][:, :TT], ALU.subtract)
                            for i, t in enumerate(tl):
                                act([bHrow[t]], [bSBF[i]], SBF[i][:], H32[:, :, t * TT:(t + 1) * TT], AF.Square)
                            for i, t in enumerate(tl):
                                pv_ = 2 * i + 1
                                mm_group([bSBF[i]], [bP[pv_]], [(PS[pv_][:, :TT], ON1024, SBF[i][:, k, :], k == 0, k == 7)
                                                                for k in range(8)])
                            for i, t in enumerate(tl):
                                pv_ = 2 * i + 1
                                rstd_from([bP[pv_]], bP[pv_], PS[pv_][:, :TT], PS[pv_][:, :TT], LN_EPS)
                            for k in range(8):
                                for i, t in enumerate(tl):
                                    t0 = t * TT
                                    dve_tt([bHrow[t], bP[2 * i + 1]], [bHrow[t]], H32[:, k, t0:t0 + TT],
                                           H32[:, k, t0:t0 + TT], PS[2 * i + 1][:, :TT], ALU.mult)
                                    act([bHrow[t]], [bHrow[t]], H32[:, k, t0:t0 + TT], H32[:, k, t0:t0 + TT], AF.Identity,
                                        scale=V_(gcol + k), bias=V_(bcol + k))
                            for i, t in enumerate(tl):
                                t0 = t * TT
                                dve_copy([bHrow[t]], [bdst[t]], dst[:, :, t0:t0 + TT], H32[:, :, t0:t0 + TT])

                    def load_w1(i, g):
                        c.dma(c.pool, bSL1[i], [], [bSL1[i]], out=SL1[i][:],
                              in_=w_ff1_d[l, :, 512 * g:512 * g + 512].rearrange("(k p) n -> p k n", p=128))

                    def load_w2(i, g):
                        c.dma(c.pool, bSL2[i], [], [bSL2[i]], out=SL2[i][:],
                              in_=w_ff2_d[l, 512 * g:512 * g + 512, :].rearrange("(j p) n -> p j n", p=128))

                    load_w1(0, 0)
                    load_w2(0, 0)
                    load_w1(1, 1)
                    load_w2(1, 1)

                    sc(f"LN1_s{s}_l{l}")
                    layer_norm_tiles(vb + VO_MIXG, vb + VO_MIXB, HB2, bHB2)

                    sc(f"FFN_s{s}_l{l}")
                    seq = [(g, t) for g in range(8) for t in range(NT)]

                    def ffn_w1(idx):
                        g, t = seq[idx]
                        i, t0, hi = g % 2, t * TT, idx % 2
                        for j in range(4):
                            mm_group([bHB2[t], bSL1[i]], [bP[j]],
                                     [(PS[j][:, :TT], SL1[i][:, k, 128 * j:128 * j + 128], HB2[:, k, t0:t0 + TT],
                                       k == 0, k == 7) for k in range(8)])
                            act([bP[j]], [bRL[j]], RL[j][:], PS[j][:, :TT], AF.Relu)
                            dve_tt([bRL[j], bP[j]], [bHID[hi]], HID[hi][:, j, :], PS[j][:, :TT], RL[j][:], ALU.mult)

                    def ffn_w2(idx):
                        g, t = seq[idx]
                        i, t0, hi = g % 2, t * TT, idx % 2
                        for m in range(8):
                            pf = 4 + m % 2
                            mm_group([bHID[hi], bSL2[i]], [bP[pf]],
                                     [(PS[pf][:, :TT], SL2[i][:, j, 128 * m:128 * m + 128], HID[hi][:, j, :],
                                       j == 0, j == 3) for j in range(4)])
                            dve_stt([bP[pf], bHrow[t]], [bHrow[t]], H32[:, m, t0:t0 + TT], H32[:, m, t0:t0 + TT],
                                    (ALPHA if g == 0 else 1.0), PS[pf][:, :TT], ALU.mult, ALU.add)

                    ffn_w1(0)
                    for idx in range(len(seq)):
                        if idx + 1 < len(seq):
                            ffn_w1(idx + 1)
                        ffn_w2(idx)
                        g, t = seq[idx]
                        if t == NT - 1 and g + 2 < 8:
                            load_w1(g % 2, g + 2)
                            load_w2(g % 2, g + 2)
                    sc(f"LN2_s{s}_l{l}")
                    layer_norm_tiles(vb + VO_FFG, vb + VO_FFB, HB, bHBo)
                    c.barrier()

            sc(f"out_s{s}")
            with ExitStack() as st:
                OUTT = [st.enter_context(nc.sbuf_tensor(nm("OUTT"), [128, D], F32)) for _ in range(2)]
                bO = [Buf("OUT_0"), Buf("OUT_1")]
                bP = [Buf(f"P{i}") for i in range(8)]
                bH = Buf("H32all")
                for blk in range(1, NB):
                    k0 = BO[blk]
                    i = blk % 2
                    for half in range(2):
                        pb = 2 * i + half
                        def emit_t(half=half, pb=pb):
                            last = None
                            for j in range(4):
                                ch = half * 4 + j
                                last = nc.tensor.transpose(PS[pb][:, j * 128:(j + 1) * 128], H32[:, ch, k0:k0 + 128], IDENT)
                            return last
                        c.op(c.pe, [bH], [bP[pb]], emit_t)
                        if half == 0:
                            act([bP[pb]], [bO[i]], OUTT[i][:, 0:512], PS[pb][:, :], AF.Copy)
                        else:
                            dve_copy([bP[pb]], [bO[i]], OUTT[i][:, 512:1024], PS[pb][:, :])
                    c.dma(c.sp, bO[i], [bO[i]], [], out=out_d[s, (blk - 1) * 128: blk * 128, :], in_=OUTT[i][:])
                c.barrier()
        sc(None)
    return nc


def _consts():
    ident = np.eye(128, dtype=np.float32)
    j = np.arange(128)[:, None]
    s_ = np.arange(128)[None, :]
    mask = (s_ > j).astype(np.float32)
    ui = (j >= s_).astype(np.float32)
    lx = (j < s_).astype(np.float32)
    o = np.ones((128, 128), np.float32)
    cf32 = np.concatenate([ident, mask], axis=1)
    cb16 = np.concatenate([ui, lx, o / 1024.0, o / 512.0, o / 256.0], axis=1).astype(ml_dtypes.bfloat16)
    return np.ascontiguousarray(cf32), np.ascontiguousarray(cb16)


def _vecs(inp):
    v = np.zeros((128, NV), np.float32)

    def col(a):
        return np.asarray(a, np.float32).reshape(-1, 128).T

    v[:, VC_LNIN_G:VC_LNIN_G + 8] = col(inp["ln_in_g"])
    v[:, VC_LNIN_B:VC_LNIN_B + 8] = col(inp["ln_in_b"])
    for l in range(DEPTH):
        b = VC_L0 + VC_PER * l
        v[:, b + VO_MIXG:b + VO_MIXG + 8] = col(inp["ln_mix_g"][l])
        v[:, b + VO_MIXB:b + VO_MIXB + 8] = col(inp["ln_mix_b"][l])
        v[:, b + VO_FFG:b + VO_FFG + 8] = col(inp["ln_ff_g"][l])
        v[:, b + VO_FFB:b + VO_FFB + 8] = col(inp["ln_ff_b"][l])
        v[:, b + VO_GMIX:b + VO_GMIX + 8] = col(inp["g_mix"][l])
        wc = np.asarray(inp["w_conf_dw"][l], np.float32)
        for cc in range(2):
            v[:, b + VO_CW + cc * 31: b + VO_CW + (cc + 1) * 31] = wc[:, cc * 128:(cc + 1) * 128].T
        v[:, b + VO_CB:b + VO_CB + 2] = col(inp["b_conf_dw"][l])
        v[:, b + VO_CG:b + VO_CG + 2] = col(inp["ln_conf_g"][l])
        v[:, b + VO_CBB:b + VO_CBB + 2] = col(inp["ln_conf_b"][l])
        ws = np.asarray(inp["w_short_dw"][l], np.float32)
        for cc in range(2):
            v[:, b + VO_SW + cc * 3: b + VO_SW + (cc + 1) * 3] = ws[:, cc * 128:(cc + 1) * 128].T
    return v


_NC_CACHE = {}


def run(inputs, nl=DEPTH, nseq=NSEQ, ncores=NCORES, trace=False):
    key = (nl, nseq)
    if key not in _NC_CACHE:
        _NC_CACHE[key] = build(nl, nseq)
    nc = _NC_CACHE[key]
    cf32, cb16 = _consts()
    vecs = _vecs(inputs)
    x = np.asarray(inputs["x"], np.float32)
    shared = {
        "meta": np.ascontiguousarray(inputs["meta_tokens"], np.float32),
        "w_in": np.ascontiguousarray(inputs["w_in"], np.float32),
        "w_out": np.ascontiguousarray(inputs["w_out"], np.float32),
        "w_ff1": np.ascontiguousarray(inputs["w_ff1"], np.float32),
        "w_ff2": np.ascontiguousarray(inputs["w_ff2"], np.float32),
        "vecs": vecs, "cf32": cf32, "cb16": cb16,
    }
    in_maps = []
    for i in range(ncores):
        m = dict(shared)
        m["x"] = np.ascontiguousarray(x[i * nseq:(i + 1) * nseq])
        in_maps.append(m)
    res = run_bass_kernel_spmd(nc, in_maps, core_ids=list(range(ncores)), **({"trace": True} if trace else {}))
    out = np.concatenate([r["out"] for r in res.results], axis=0)
    return out, res


def kernel(**inputs):
    out, _ = run(inputs)
    return out.astype(np.float32)
```

```python
from contextlib import ExitStack
import numpy as np
import ml_dtypes
import concourse.bass as bass
import concourse.mybir as mybir
from concourse.bass_utils import run_bass_kernel_spmd

F32 = mybir.dt.float32
BF16 = mybir.dt.bfloat16
AF = mybir.ActivationFunctionType
ALU = mybir.AluOpType
AX = mybir.AxisListType

DEPTH = 4
D = 1024
SEQ = 2048
NMETA = 16
L = SEQ + NMETA
DIN = 2816
DFF = 4096
ALPHA = (2.0 * DEPTH) ** 0.25
LN_EPS = 1e-5
RMS_EPS = 1e-6
TT = 344
NT = 6
BS = [16] + [128] * 16
BO = [0] + [16 + 128 * i for i in range(16)]
NB = 17
ABS = [128] * 16 + [16]
ABO = [128 * i for i in range(17)]
EPOCH = 30000
NCORES = 8
NSEQ = 2

VC_LNIN_G, VC_LNIN_B = 0, 8
VC_L0 = 16
VC_PER = 114
VO_MIXG, VO_MIXB, VO_FFG, VO_FFB, VO_GMIX = 0, 8, 16, 24, 32
VO_CW = 40
VO_CB = 102
VO_CG = 104
VO_CBB = 106
VO_SW = 108
NV = VC_L0 + VC_PER * DEPTH


class Buf:
    __slots__ = ("name", "lw", "rd", "dsem", "dcount")

    def __init__(self, name):
        self.name = name
        self.lw = None
        self.rd = []
        self.dsem = None
        self.dcount = 0


class Eng:
    def __init__(self, ctx, name, h):
        self.ctx = ctx
        self.name = name
        self.h = h
        self.n = 0
        self.sems = []
        self.seen = {}

    def sem_val(self, idx):
        ep = idx // EPOCH
        while len(self.sems) <= ep:
            self.sems.append(self.ctx.nc.alloc_semaphore(f"s_{self.name}_{len(self.sems)}"))
        return self.sems[ep], idx % EPOCH + 1


class Ctx:
    def __init__(self, nc):
        self.nc = nc
        self.pe = Eng(self, "pe", nc.tensor)
        self.act = Eng(self, "act", nc.scalar)
        self.dve = Eng(self, "dve", nc.vector)
        self.pool = Eng(self, "pool", nc.gpsimd)
        self.sp = Eng(self, "sp", nc.sync)
        self.engs = [self.pe, self.act, self.dve, self.pool, self.sp]
        self.owners = []
        self.dreg = {}
        self.log = None

    def _wait(self, eng, tick):
        key, idx = tick
        if eng.seen.get(key, -1) >= idx:
            return
        if isinstance(key, Eng):
            sem, val = key.sem_val(idx)
        else:
            sem, val = key.dsem, 16 * (idx + 1)
        eng.h.wait_ge(sem, val)
        if self.log is not None:
            self.log.append((eng.name, "wait", id(sem), val))
        eng.seen[key] = idx

    @staticmethod
    def _deps(reads, writes):
        deps = []
        for b in reads:
            if b.lw is not None:
                deps.append(b.lw)
        for b in writes:
            if b.lw is not None:
                deps.append(b.lw)
            deps.extend(b.rd)
        return deps

    @staticmethod
    def _record(tick, reads, writes):
        for b in reads:
            b.rd.append(tick)
        for b in writes:
            b.lw = tick
            b.rd = []

    def op(self, eng, reads, writes, emit):
        for t in self._deps(reads, writes):
            if t[0] is eng and eng is self.pe:
                continue
            self._wait(eng, t)
        inst = emit()
        idx = eng.n
        eng.n += 1
        sem, _ = eng.sem_val(idx)
        inst.then_inc(sem, 1)
        if self.log is not None:
            self.log.append((eng.name, "inc", id(sem), 1))
        self._record((eng, idx), reads, writes)

    def dma(self, eng, owner, reads, writes, out, in_, **kw):
        if owner.dsem is None:
            if owner.name in self.dreg:
                owner.dsem, owner.dcount = self.dreg[owner.name]
            else:
                owner.dsem = self.nc.alloc_semaphore(f"d_{owner.name}")
            self.owners = [o for o in self.owners if o.name != owner.name] + [owner]
        deps = self._deps(reads, writes)
        if owner.dcount > 0:
            deps.append((owner, owner.dcount - 1))
        for t in deps:
            self._wait(eng, t)
        inst = eng.h.dma_start(out=out, in_=in_, **kw)
        idx = owner.dcount
        owner.dcount += 1
        inst.then_inc(owner.dsem, 16)
        if self.log is not None:
            self.log.append((eng.name, "dmainc", id(owner.dsem), 16))
        self.dreg[owner.name] = (owner.dsem, owner.dcount)
        self._record((owner, idx), reads, writes)

    def barrier(self):
        ticks = [(e, e.n - 1) for e in self.engs if e.n > 0]
        ticks += [(o, o.dcount - 1) for o in self.owners if o.dcount > 0]
        for e in self.engs:
            for t in ticks:
                self._wait(e, t)


def build(nl=DEPTH, nseq=NSEQ, log=None):
    nc = bass.Bass("TRN2", target_bir_lowering=False)

    def din(name, shape, dt=F32):
        return nc.dram_tensor(name, list(shape), dt, kind="ExternalInput").ap()

    x_d = din("x", [nseq, SEQ, D])
    meta_d = din("meta", [NMETA, D])
    w_in_d = din("w_in", [DEPTH, D, DIN])
    w_out_d = din("w_out", [DEPTH, D, D])
    w_ff1_d = din("w_ff1", [DEPTH, D, DFF])
    w_ff2_d = din("w_ff2", [DEPTH, DFF, D])
    vecs_d = din("vecs", [128, NV])
    cf32_d = din("cf32", [128, 256])
    cb16_d = din("cb16", [128, 640], BF16)
    out_d = nc.dram_tensor("out", [nseq, SEQ, D], F32, kind="ExternalOutput").ap()

    c = Ctx(nc)
    c.log = log
    uid = [0]

    def nm(s):
        uid[0] += 1
        return f"{s}_{uid[0]}"

    cur_scope = [None]

    def sc(name):
        if cur_scope[0] is not None:
            nc.leave_named_scope(cur_scope[0][0], cur_scope[0][1], False)
            cur_scope[0] = None
        if name is not None:
            sid, _ = nc.enter_named_scope(name, False)
            cur_scope[0] = (name, sid)

    H32 = nc.alloc_sbuf_tensor("H32", [128, 8, L], F32)
    HB = nc.alloc_sbuf_tensor("HB", [128, 8, L], BF16)
    VEC = nc.alloc_sbuf_tensor("VEC", [128, NV], F32)
    CF = nc.alloc_sbuf_tensor("CF", [128, 256], F32)
    CB = nc.alloc_sbuf_tensor("CB", [128, 640], BF16)
    IDENT = CF[:, 0:128]
    MASK = CF[:, 128:256]
    UI = CB[:, 0:128]
    LX = CB[:, 128:256]
    ON1024 = CB[:, 256:384]
    ON512 = CB[:, 384:512]
    ON256 = CB[:, 512:640]
    PS = [nc.alloc_psum_tensor(f"ps{i}", [128, 512], F32) for i in range(8)]

    def V_(col):
        return VEC[:, col:col + 1]

    def act(reads, writes, out, in_, func, scale=None, bias=None):
        kw = {}
        if scale is not None:
            kw["scale"] = scale
        if bias is not None:
            kw["bias"] = bias
        c.op(c.act, reads, writes, lambda: nc.scalar.activation(out=out, in_=in_, func=func, **kw))

    def dve_tt(reads, writes, out, in0, in1, op):
        c.op(c.dve, reads, writes, lambda: nc.vector.tensor_tensor(out=out, in0=in0, in1=in1, op=op))

    def dve_ts(reads, writes, out, in0, s1, s2, op0, op1=None):
        if op1 is None:
            c.op(c.dve, reads, writes,
                 lambda: nc.vector.tensor_scalar(out=out, in0=in0, scalar1=s1, scalar2=None, op0=op0))
        else:
            c.op(c.dve, reads, writes,
                 lambda: nc.vector.tensor_scalar(out=out, in0=in0, scalar1=s1, scalar2=s2, op0=op0, op1=op1))

    def dve_stt(reads, writes, out, in0, scalar, in1, op0, op1):
        c.op(c.dve, reads, writes,
             lambda: nc.vector.scalar_tensor_tensor(out=out, in0=in0, scalar=scalar, in1=in1, op0=op0, op1=op1))

    def dve_copy(reads, writes, out, in_):
        c.op(c.dve, reads, writes, lambda: nc.vector.tensor_copy(out=out, in_=in_))

    def dve_recip(reads, writes, out, in_):
        c.op(c.dve, reads, writes, lambda: nc.vector.reciprocal(out=out, in_=in_))

    def mm_group(reads, writes, mms):
        def emit():
            last = None
            for (o, l_, r_, st, sp_) in mms:
                last = nc.tensor.matmul(o, lhsT=l_, rhs=r_, start=st, stop=sp_, skip_group_check=True)
            return last
        c.op(c.pe, reads, writes, emit)

    def rstd_from(reads_b, out_b, out_ap, in_ap, eps):
        act(reads_b, [out_b], out_ap, in_ap, AF.Ln, bias=eps)
        act([out_b], [out_b], out_ap, out_ap, AF.Exp, scale=-0.5)

    with nc.Block():
        bV, bCF, bCB = Buf("VEC"), Buf("CF"), Buf("CB")
        c.dma(c.sp, bV, [], [bV], out=VEC[:], in_=vecs_d[:, :])
        c.dma(c.sp, bCF, [], [bCF], out=CF[:], in_=cf32_d[:, :])
        c.dma(c.sp, bCB, [], [bCB], out=CB[:], in_=cb16_d[:, :])
        c.barrier()

        for s in range(nseq):
            sc(f"p0_s{s}")
            with ExitStack() as st:
                NP0 = 3
                XT = [st.enter_context(nc.sbuf_tensor(nm("XT"), [128, D], F32)) for _ in range(NP0)]
                SQT = [st.enter_context(nc.sbuf_tensor(nm("SQT"), [128, D], F32)) for _ in range(NP0)]
                STAT = [st.enter_context(nc.sbuf_tensor(nm("STAT"), [128, 4], F32)) for _ in range(NP0)]
                bXT = [Buf(f"XT{i}") for i in range(NP0)]
                bSQ = [Buf(f"SQT{i}") for i in range(NP0)]
                bST = [Buf(f"ST{i}") for i in range(NP0)]
                bP = [Buf(f"P{i}") for i in range(8)]
                bHblk = [Buf(f"H32_{j}") for j in range(NB)]
                bHBblk = [Buf(f"HB_{j}") for j in range(NB)]

                def p0_block(blk):
                    bs, k0 = BS[blk], BO[blk]
                    i = blk % NP0
                    xt, stt, sqt = XT[i], STAT[i], SQT[i]
                    src = meta_d[:, :] if blk == 0 else x_d[s, (blk - 1) * 128: blk * 128, :]
                    c.dma(c.sp, bXT[i], [], [bXT[i]], out=xt[:bs, :], in_=src)
                    yield
                    c.op(c.dve, [bXT[i]], [bST[i]],
                         lambda: nc.vector.reduce_sum(out=stt[:bs, 0:1], in_=xt[:bs, :], axis=AX.X))
                    yield
                    dve_ts([bST[i]], [bST[i]], stt[:bs, 1:2], stt[:bs, 0:1], -1.0 / D, None, ALU.mult)
                    yield
                    dve_ts([bXT[i], bST[i]], [bXT[i]], xt[:bs, :], xt[:bs, :], stt[:bs, 1:2], None, ALU.add)
                    yield
                    act([bXT[i]], [bSQ[i]], sqt[:bs, :], xt[:bs, :], AF.Square)
                    yield
                    c.op(c.dve, [bSQ[i]], [bST[i]],
                         lambda: nc.vector.reduce_sum(out=stt[:bs, 2:3], in_=sqt[:bs, :], axis=AX.X))
                    yield
                    act([bST[i]], [bST[i]], stt[:bs, 3:4], stt[:bs, 2:3], AF.Ln, scale=1.0 / D, bias=LN_EPS)
                    act([bST[i]], [bST[i]], stt[:bs, 3:4], stt[:bs, 3:4], AF.Exp, scale=-0.5)
                    yield
                    dve_ts([bXT[i], bST[i]], [bXT[i]], xt[:bs, :], xt[:bs, :], stt[:bs, 3:4], None, ALU.mult)
                    yield
                    for half in range(2):
                        pb = 2 * i + half

                        def emit_t(half=half, pb=pb):
                            last = None
                            for j in range(4):
                                ch = half * 4 + j
                                last = nc.tensor.transpose(PS[pb][:, j * 128: j * 128 + bs],
                                                           xt[:bs, ch * 128:(ch + 1) * 128], IDENT[:bs, :bs])
                            return last
                        c.op(c.pe, [bXT[i]], [bP[pb]], emit_t)
                        yield
                        for j in range(4):
                            ch = half * 4 + j
                            act([bP[pb]], [bHblk[blk]], H32[:, ch, k0:k0 + bs], PS[pb][:, j * 128: j * 128 + bs],
                                AF.Identity, scale=V_(VC_LNIN_G + ch), bias=V_(VC_LNIN_B + ch))
                        yield
                    dve_copy([bHblk[blk]], [bHBblk[blk]], HB[:, :, k0:k0 + bs], H32[:, :, k0:k0 + bs])
                    yield

                active, nxt = [], 0
                while active or nxt < NB:
                    if nxt < NB and len(active) < NP0:
                        active.append(p0_block(nxt))
                        nxt += 1
                    for gen in list(active):
                        try:
                            next(gen)
                        except StopIteration:
                            active.remove(gen)
                c.barrier()

            for l in range(nl):
                vb = VC_L0 + VC_PER * l
                with ExitStack() as stA:
                    YC = stA.enter_context(nc.sbuf_tensor(nm("YC"), [128, 4, L], BF16))
                    for sub in range(2):
                        sc(f"{"AC" if sub == 0 else "A1"}_s{s}_l{l}")
                        with ExitStack() as st:
                            SL = [st.enter_context(nc.sbuf_tensor(nm("SL"), [128, 4096], BF16)) for _ in range(2)]
                            bSL = [Buf("SL0"), Buf("SL1"), Buf("SL2")]
                            GLT = [st.enter_context(nc.sbuf_tensor(nm("GLT"), [128, 2, 30 + TT], BF16)) for _ in range(2)]
                            if sub == 0:
                                DM = st.enter_context(nc.sbuf_tensor(nm("DM"), [128, 62, 128], BF16))
                            bDM = Buf("DM")
                            CHT = [st.enter_context(nc.sbuf_tensor(nm("CHT"), [128, 2, 2 + TT], F32)) for _ in range(2)]
                            npar = 2 if sub == 0 else 0
                            T1 = [st.enter_context(nc.sbuf_tensor(nm("T1"), [128, 2, TT], F32)) for _ in range(npar)]
                            T2 = [st.enter_context(nc.sbuf_tensor(nm("T2"), [128, 2, TT], F32)) for _ in range(npar)]
                            T3 = [st.enter_context(nc.sbuf_tensor(nm("T3"), [128, 2, TT], F32)) for _ in range(npar)]
                            TB = [st.enter_context(nc.sbuf_tensor(nm("TB"), [128, 2, TT], BF16)) for _ in range(npar)]
                            RS = [st.enter_context(nc.sbuf_tensor(nm("RS"), [128, TT], F32)) for _ in range(npar)]
                            bGL = [Buf("GL0"), Buf("GL1")]
                            bCH = [Buf("CH0"), Buf("CH1")]
                            bT1, bT2, bT3, bTB, bRS = ([Buf(n + "a"), Buf(n + "b")] for n in ("T1", "T2", "T3", "TB", "RS"))

                            def run_interleaved(make_gen):
                                active, nxt, can_start = [], 0, True
                                while active or nxt < NT:
                                    if can_start and nxt < NT and len(active) < 2:
                                        active.append(make_gen(nxt))
                                        nxt += 1
                                        can_start = False
                                    for gen in list(active):
                                        try:
                                            if next(gen) == "halo":
                                                can_start = True
                                        except StopIteration:
                                            active.remove(gen)
                            bP = [Buf(f"P{i}") for i in range(8)]
                            bHB = Buf("HB")
                            bQ, bK, bVt, bYC = Buf("QT"), Buf("KT"), Buf("VT"), Buf("YC")
                            prr = [0]

                            def nextps():
                                i = prr[0] % (4 if sub == 0 else 6)
                                prr[0] += 1
                                return i

                            def slabview(i, w):
                                return SL[i][:, 0:8 * w].rearrange("p (k n) -> p k n", k=8)

                            def load_slab(i, w, parts):
                                sv = slabview(i, w)
                                for (off, c0, n) in parts:
                                    c.dma(c.pool, bSL[i], [], [bSL[i]], out=sv[:, :, off:off + n],
                                          in_=w_in_d[l, :, c0:c0 + n].rearrange("(k p) n -> p k n", p=128))

                            units = [("conf",), ("sc",)] if sub == 0 else [("qk", p) for p in range(4)] + [("v",)]

                            def unit_loads(u, i):
                                if u[0] == "qk":
                                    p = u[1]
                                    load_slab(i, 256, [(0, 128 * p, 128), (128, 512 + 128 * p, 128)])
                                elif u[0] == "v":
                                    load_slab(i, 512, [(0, 1024, 512)])
                                elif u[0] == "conf":
                                    load_slab(i, 512, [(0, 1536, 512)])

                            def dense_mm(pi, sv, m0, t0, n):
                                mm_group([bHB, bSLcur[0]], [bP[pi]],
                                         [(PS[pi][:, :n], sv[:, k, m0:m0 + 128], HB[:, k, t0:t0 + n], k == 0, k == 7)
                                          for k in range(8)])

                            bSLcur = [None]
                            unit_loads(units[0], 0)
                            for ui, u in enumerate(units):
                                i = ui % 2
                                if u[0] != "sc":
                                    if ui + 1 < len(units) and units[ui + 1][0] != "sc":
                                        unit_loads(units[ui + 1], (ui + 1) % 2)
                                    bSLcur[0] = bSL[i]
                                if u[0] == "qk":
                                    p = u[1]
                                    sv = slabview(i, 256)
                                    for t in range(NT):
                                        t0 = t * TT
                                        pq = nextps()
                                        dense_mm(pq, sv, 0, t0, TT)
                                        act([bP[pq]], [bQ], QT[:, p, t0:t0 + TT], PS[pq][:, :TT], AF.Identity, scale=0.125)
                                        pk = nextps()
                                        dense_mm(pk, sv, 128, t0, TT)
                                        dve_copy([bP[pk]], [bK], KT[:, p, t0:t0 + TT], PS[pk][:, :TT])
                                elif u[0] == "v":
                                    sv = slabview(i, 512)
                                    for blk in range(NB):
                                        bs, k0 = ABS[blk], ABO[blk]
                                        pv = nextps()
                                        mm_group([bHB, bSL[i]], [bP[pv]],
                                                 [(PS[pv][:bs, :], HB[:, k, k0:k0 + bs], sv[:, k, :], k == 0, k == 7)
                                                  for k in range(8)])
                                        if blk % 2 == 0:
                                            act([bP[pv]], [bVt], VT[:bs, blk, :], PS[pv][:bs, :], AF.Copy)
                                        else:
                                            dve_copy([bP[pv]], [bVt], VT[:bs, blk, :], PS[pv][:bs, :])
                                elif u[0] == "conf":
                                    sv = slabview(i, 512)
                                    for cc in range(2):
                                        for k in range(31):
                                            dve_ts([], [bDM], DM[:, cc * 31 + k, :], IDENT, V_(vb + VO_CW + cc * 31 + k), None,
                                                   ALU.mult)

                                    def conf_tile(t):
                                        t0 = t * TT
                                        g = t % 2
                                        gl = GLT[g]
                                        t1, t2, t3, tb, rs = T1[g], T2[g], T3[g], TB[g], RS[g]
                                        b1, b2, b3, bb, br = bT1[g], bT2[g], bT3[g], bTB[g], bRS[g]
                                        pm, pv_, pr = (6, 7, 6) if g == 0 else (4, 5, 4)
                                        if t == 0:
                                            c.op(c.dve, [], [bGL[g]], lambda: nc.vector.memset(gl[:, :, 0:30], 0.0))
                                        for cc in range(2):
                                            pg = nextps()
                                            dense_mm(pg, sv, 256 + 128 * cc, t0, TT)
                                            pa = nextps()
                                            dense_mm(pa, sv, 128 * cc, t0, TT)
                                            yield
                                            act([bP[pg]], [b1], t1[:, cc, :], PS[pg][:, :TT], AF.Exp, scale=-1.0)
                                            act([b1], [b1], t1[:, cc, :], t1[:, cc, :], AF.Ln, bias=1.0)
                                            act([b1], [b1], t1[:, cc, :], t1[:, cc, :], AF.Exp, scale=-1.0)
                                            yield
                                            dve_tt([b1, bP[pa]], [bGL[g]], gl[:, cc, 30:30 + TT], PS[pa][:, :TT],
                                                   t1[:, cc, :], ALU.mult)
                                            yield
                                        if t + 1 < NT:
                                            gn = GLT[(t + 1) % 2]
                                            dve_copy([bGL[g]], [bGL[(t + 1) % 2]], gn[:, :, 0:30], gl[:, :, TT:TT + 30])
                                        yield "halo"
                                        for cc in range(2):
                                            pc = nextps()
                                            mm_group([bGL[g], bDM], [bP[pc]],
                                                     [(PS[pc][:, :TT], DM[:, cc * 31 + k, :], gl[:, cc, k:k + TT], k == 0, k == 30)
                                                      for k in range(31)])
                                            yield
                                            act([bP[pc]], [b2], t2[:, cc, :], PS[pc][:, :TT], AF.Identity,
                                                bias=V_(vb + VO_CB + cc))
                                            yield
                                        act([b2], [bb], tb[:], t2[:], AF.Copy)
                                        yield
                                        mm_group([bb], [bP[pm]], [(PS[pm][:, :TT], ON256, tb[:, cc, :], cc == 0, cc == 1)
                                                                  for cc in range(2)])
                                        yield
                                        for cc in range(2):
                                            dve_tt([b2, bP[pm]], [b2], t2[:, cc, :], t2[:, cc, :], PS[pm][:, :TT], ALU.subtract)
                                        yield
                                        act([b2], [bb], tb[:], t2[:], AF.Square)
                                        yield
                                        mm_group([bb], [bP[pv_]], [(PS[pv_][:, :TT], ON256, tb[:, cc, :], cc == 0, cc == 1)
                                                                   for cc in range(2)])
                                        yield
                                        rstd_from([bP[pv_]], br, rs[:], PS[pv_][:, :TT], LN_EPS)
                                        yield
                                        for cc in range(2):
                                            dve_tt([b2, br], [b2], t2[:, cc, :], t2[:, cc, :], rs[:], ALU.mult)
                                            yield
                                            act([b2], [b2], t2[:, cc, :], t2[:, cc, :], AF.Identity,
                                                scale=V_(vb + VO_CG + cc), bias=V_(vb + VO_CBB + cc))
                                            yield
                                        act([b2], [b3], t3[:], t2[:], AF.Exp, scale=-1.0)
                                        act([b3], [b3], t3[:], t3[:], AF.Ln, bias=1.0)
                                        act([b3], [b3], t3[:], t3[:], AF.Exp, scale=-1.0)
                                        yield
                                        dve_tt([b2, b3], [b2], t2[:], t2[:], t3[:], ALU.mult)
                                        yield
                                        act([b2], [bb], tb[:], t2[:], AF.Square)
                                        yield
                                        mm_group([bb], [bP[pr]], [(PS[pr][:, :TT], ON256, tb[:, cc, :], cc == 0, cc == 1)
                                                                  for cc in range(2)])
                                        yield
                                        rstd_from([bP[pr]], br, rs[:], PS[pr][:, :TT], RMS_EPS)
                                        yield
                                        for cc in range(2):
                                            dve_stt([b2, br], [bYC], YC[:, cc, t0:t0 + TT], t2[:, cc, :],
                                                    V_(vb + VO_GMIX + 4 + cc), rs[:], ALU.mult, ALU.mult)
                                        yield

                                    run_interleaved(conf_tile)
                                elif u[0] == "sc":
                                    load_slab(0, 512, [(0, 2048, 512)])
                                    load_slab(1, 256, [(0, 2560, 256)])
                                    svA = slabview(0, 512)
                                    svB = slabview(1, 256)

                                    def sc_tile(t):
                                        t0 = t * TT
                                        g = t % 2
                                        ch = CHT[g]
                                        t1, t2, t3, tb, rs = T1[g], T2[g], T3[g], TB[g], RS[g]
                                        b1, b2, b3, bb, br = bT1[g], bT2[g], bT3[g], bTB[g], bRS[g]
                                        pr = 7 if g == 0 else 5
                                        if t == 0:
                                            c.op(c.dve, [], [bCH[g]], lambda: nc.vector.memset(ch[:, :, 0:2], 0.0))
                                        for cc in range(2):
                                            pB = nextps()
                                            bSLcur[0] = bSL[0]
                                            dense_mm(pB, svA, 128 * cc, t0, TT)
                                            pC = nextps()
                                            dense_mm(pC, svA, 256 + 128 * cc, t0, TT)
                                            pH = nextps()
                                            bSLcur[0] = bSL[1]
                                            dense_mm(pH, svB, 128 * cc, t0, TT)
                                            yield
                                            act([bP[pC]], [b1], t1[:, cc, :], PS[pC][:, :TT], AF.Copy)
                                            act([bP[pB]], [b3], t3[:, cc, :], PS[pB][:, :TT], AF.Copy)
                                            yield
                                            dve_tt([b1, bP[pH]], [bCH[g]], ch[:, cc, 2:2 + TT], PS[pH][:, :TT], t1[:, cc, :],
                                                   ALU.mult)
                                            yield
                                        if t + 1 < NT:
                                            chn = CHT[(t + 1) % 2]
                                            dve_copy([bCH[g]], [bCH[(t + 1) % 2]], chn[:, :, 0:2], ch[:, :, TT:TT + 2])
                                        yield "halo"
                                        for cc in range(2):
                                            wc = vb + VO_SW + cc * 3
                                            dve_ts([bCH[g]], [b2], t2[:, cc, :], ch[:, cc, 0:TT], V_(wc), None, ALU.mult)
                                            yield
                                            for k in range(1, 3):
                                                dve_stt([bCH[g], b2], [b2], t2[:, cc, :], ch[:, cc, k:k + TT], V_(wc + k),
                                                        t2[:, cc, :], ALU.mult, ALU.add)
                                                yield
                                            dve_tt([b2, b3], [b2], t2[:, cc, :], t2[:, cc, :], t3[:, cc, :], ALU.mult)
                                            yield
                                        act([b2], [bb], tb[:], t2[:], AF.Square)
                                        yield
                                        mm_group([bb], [bP[pr]], [(PS[pr][:, :TT], ON256, tb[:, cc, :], cc == 0, cc == 1)
                                                                  for cc in range(2)])
                                        yield
                                        rstd_from([bP[pr]], br, rs[:], PS[pr][:, :TT], RMS_EPS)
                                        yield
                                        for cc in range(2):
                                            dve_stt([b2, br], [bYC], YC[:, 2 + cc, t0:t0 + TT], t2[:, cc, :],
                                                    V_(vb + VO_GMIX + 6 + cc), rs[:], ALU.mult, ALU.mult)
                                        yield

                                    run_interleaved(sc_tile)
                            c.barrier()
                        if sub == 0:
                            QT = stA.enter_context(nc.sbuf_tensor(nm("QT"), [128, 4, L], BF16))
                            KT = stA.enter_context(nc.sbuf_tensor(nm("KT"), [128, 4, L], BF16))
                            VT = stA.enter_context(nc.sbuf_tensor(nm("VT"), [128, NB, 512], BF16))

                    sc(f"A2_s{s}_l{l}")
                    with ExitStack() as st:
                        E = [[st.enter_context(nc.sbuf_tensor(nm("E"), [128, 512], F32)) for _ in range(2)] for _ in range(2)]
                        SPT = [[st.enter_context(nc.sbuf_tensor(nm("SP"), [128, 512], BF16)) for _ in range(2)] for _ in range(2)]
                        WT = [st.enter_context(nc.sbuf_tensor(nm("WT"), [128, 512], F32)) for _ in range(2)]
                        AT = [[st.enter_context(nc.sbuf_tensor(nm("AT"), [128, 512], BF16)) for _ in range(2)] for _ in range(2)]
                        OT = st.enter_context(nc.sbuf_tensor(nm("OT"), [128, 4, 512], F32))
                        SQ = st.enter_context(nc.sbuf_tensor(nm("SQ"), [128, 4, 512], BF16))
                        RS = st.enter_context(nc.sbuf_tensor(nm("RSa"), [128, 512], F32))
                        bE = [[Buf("E"), Buf("E")] for _ in range(2)]
                        bSP = [[Buf("SP"), Buf("SP")] for _ in range(2)]
                        bW = [Buf("W"), Buf("W")]
                        bA = [[Buf("A"), Buf("A")] for _ in range(2)]
                        bOT, bSQ, bRS = Buf("OT"), Buf("SQ"), Buf("RS")
                        bZ = [[Buf("Z"), Buf("Z")] for _ in range(2)]
                        bACC = [Buf("ACC"), Buf("ACC")]
                        bPO = [Buf("PO"), Buf("PO")]
                        bQK, bVt, bY = Buf("QK"), Buf("VT"), Buf("Y")
                        Z = [[PS[0], PS[1]], [PS[2], PS[3]]]
                        ACC = [PS[4], PS[5]]
                        PO = [PS[6], PS[7]]
                        cnt = [0, 0]
                        sweep_no = [0]

                        groups = [(0, NMETA)] + [(NMETA + 512 * g, 512) for g in range(4)]
                        for gi, (q0, NQ) in enumerate(groups):
                            items = []
                            for p in range(4):
                                if gi == 0:
                                    steps = [(0, 0, 16, (0, 16))]
                                else:
                                    base = 4 * (gi - 1)
                                    steps = []
                                    for j in (4, 3, 2, 1, 0):
                                        lo_j = max(0, 128 * j - 16)
                                        mk = (16, 112) if j == 0 else (0, min(128, NQ - lo_j))
                                        steps.append((base + j, lo_j, ABS[base + j], mk))
                                    steps += [(blk, 0, 128, None) for blk in range(base - 1, -1, -1)]
                                po_i = sweep_no[0] % 2
                                sweep_no[0] += 1
                                for si, (blk, lo, sbs, mk) in enumerate(steps):
                                    for h in range(2):
                                        prev = steps[si - 1] if si > 0 else None
                                        items.append(dict(p=p, h=h, blk=blk, lo=lo, bs=sbs, mask=mk, first=(si == 0),
                                                          last=(si == len(steps) - 1), prev=prev, b=None, po=po_i))
                            for it in items:
                                it["b"] = cnt[it["h"]] % 2
                                cnt[it["h"]] += 1

                            def Zs(it):
                                h, b, p, blk, lo = it["h"], it["b"], it["p"], it["blk"], it["lo"]
                                bs, k0 = it["bs"], ABO[blk]
                                mm_group([bQK], [bZ[h][b]],
                                         [(Z[h][b][:bs, lo:NQ], KT[64 * h:64 * h + 64, p, k0:k0 + bs],
                                           QT[64 * h:64 * h + 64, p, q0 + lo:q0 + NQ], True, True)])

                            def EXPs(it):
                                h, b, blk, lo = it["h"], it["b"], it["blk"], it["lo"]
                                bs = it["bs"]
                                act([bZ[h][b]], [bE[h][b]], E[h][b][:bs, lo:NQ], Z[h][b][:bs, lo:NQ], AF.Exp)
                                if it["mask"] is not None:
                                    m0, w = it["mask"]
                                    dve_tt([bE[h][b]], [bE[h][b]], E[h][b][:bs, lo:lo + w], E[h][b][:bs, lo:lo + w],
                                           MASK[:bs, m0:m0 + w], ALU.mult)

                            def LNs(it):
                                h, b, blk, lo = it["h"], it["b"], it["blk"], it["lo"]
                                bs = it["bs"]
                                act([bE[h][b]], [bSP[h][b]], SPT[h][b][:bs, lo:NQ], E[h][b][:bs, lo:NQ], AF.Ln, bias=1.0)

                            def CUMs(it):
                                h, b, blk, lo = it["h"], it["b"], it["blk"], it["lo"]
                                bs = it["bs"]
                                mms = []
                                reads = [bSP[h][b]]
                                if it["prev"] is not None:
                                    pblk, plo, pbs, _ = it["prev"]
                                    mms.append((ACC[h][:bs, plo:NQ], LX[:pbs, :bs], SPT[h][1 - b][:pbs, plo:NQ], False, False))
                                    reads.append(bSP[h][1 - b])
                                if it["first"]:
                                    mms.append((ACC[h][:, lo:NQ], UI[:bs, :128], SPT[h][b][:bs, lo:NQ], True, True))
                                else:
                                    mms.append((ACC[h][:bs, lo:NQ], UI[:bs, :bs], SPT[h][b][:bs, lo:NQ], False, True))
                                mm_group(reads, [bACC[h]], mms)

                            def EXPNEGs(it):
                                h, b, blk, lo = it["h"], it["b"], it["blk"], it["lo"]
                                bs = it["bs"]
                                act([bACC[h]], [bW[h]], WT[h][:bs, lo:NQ], ACC[h][:bs, lo:NQ], AF.Exp, scale=-1.0)
                                dve_tt([bE[h][b], bW[h]], [bA[h][b]], AT[h][b][:bs, lo:NQ], E[h][b][:bs, lo:NQ],
                                       WT[h][:bs, lo:NQ], ALU.mult)

                            def AVs(it):
                                h, b, blk, lo = it["h"], it["b"], it["blk"], it["lo"]
                                bs = it["bs"]
                                hh = 2 * it["p"] + h
                                po = it["po"]
                                mm_group([bA[h][b], bVt], [bPO[po]],
                                         [(PO[po][64 * h:64 * h + 64, lo:NQ], VT[:bs, blk, 64 * hh:64 * hh + 64],
                                           AT[h][b][:bs, lo:NQ], it["first"], it["last"])])
                                if it["last"] and h == 1:
                                    act([bPO[po]], [bOT], OT[:, it["p"], :NQ], PO[po][:, :NQ], AF.Copy)

                            NP = len(items) // 2

                            def both(fn, r):
                                if 0 <= r < NP:
                                    fn(items[2 * r])
                                    fn(items[2 * r + 1])

                            for r in range(-2, NP):
                                both(Zs, r + 2)
                                both(EXPs, r + 1)
                                both(EXPNEGs, r)
                                both(LNs, r + 1)
                                both(CUMs, r + 1)
                                both(AVs, r)
                            pb_ = sweep_no[0] % 2
                            act([bOT], [bSQ], SQ[:, :, :NQ], OT[:, :, :NQ], AF.Square)
                            mm_group([bSQ], [bPO[pb_]], [(PO[pb_][:, :NQ], ON512, SQ[:, p, :NQ], p == 0, p == 3)
                                                         for p in range(4)])
                            rstd_from([bPO[pb_]], bRS, RS[:, :NQ], PO[pb_][:, :NQ], RMS_EPS)
                            for p in range(4):
                                dve_stt([bOT, bRS], [bY], HB[:, p, q0:q0 + NQ], OT[:, p, :NQ], V_(vb + VO_GMIX + p),
                                        RS[:, :NQ], ALU.mult, ALU.mult)
                        c.barrier()

                    sc(f"B_s{s}_l{l}")
                    stB = ExitStack()
                    SLb = [stB.enter_context(nc.sbuf_tensor(nm("SLb"), [128, 8, 256], BF16)) for _ in range(2)]
                    bSLb = [Buf("SLb0"), Buf("SLb1")]
                    bHrow = [Buf(f"H{t}") for t in range(NT)]
                    bYall = Buf("Yall")
                    bP = [Buf(f"P{i}") for i in range(8)]

                    def load_wout(i, mp):
                        c.dma(c.pool, bSLb[i], [], [bSLb[i]], out=SLb[i][:],
                              in_=w_out_d[l, :, 256 * mp:256 * mp + 256].rearrange("(k p) n -> p k n", p=128))

                    load_wout(0, 0)
                    pi = 0
                    for mp in range(4):
                        i = mp % 2
                        if mp + 1 < 4:
                            load_wout((mp + 1) % 2, mp + 1)
                        for mm_ in range(2):
                            m = 2 * mp + mm_
                            for t in range(NT):
                                t0 = t * TT
                                pp = pi % 4
                                pi += 1
                                mm_group([bYall, bSLb[i]], [bP[pp]],
                                         [(PS[pp][:, :TT], SLb[i][:, k, 128 * mm_:128 * mm_ + 128],
                                           (HB[:, k, t0:t0 + TT] if k < 4 else YC[:, k - 4, t0:t0 + TT]), k == 0, k == 7)
                                          for k in range(8)])
                                dve_stt([bP[pp], bHrow[t]], [bHrow[t]], H32[:, m, t0:t0 + TT], H32[:, m, t0:t0 + TT],
                                        ALPHA, PS[pp][:, :TT], ALU.mult, ALU.add)
                    c.barrier()
                    stB.close()
                with ExitStack() as st:
                    HB2 = st.enter_context(nc.sbuf_tensor(nm("HB2"), [128, 8, L], BF16))
                    SBF = [st.enter_context(nc.sbuf_tensor(nm("SBF"), [128, 8, TT], BF16)) for _ in range(3)]
                    SL1 = [st.enter_context(nc.sbuf_tensor(nm("SL1"), [128, 8, 512], BF16)) for _ in range(2)]
                    SL2 = [st.enter_context(nc.sbuf_tensor(nm("SL2"), [128, 4, 1024], BF16)) for _ in range(2)]
                    HID = [st.enter_context(nc.sbuf_tensor(nm("HID"), [128, 4, TT], BF16)) for _ in range(2)]
                    RL = [st.enter_context(nc.sbuf_tensor(nm("RL"), [128, TT], F32)) for _ in range(4)]
                    bSBF = [Buf("SBF0"), Buf("SBF1"), Buf("SBF2")]
                    bSL1 = [Buf("SL1a"), Buf("SL1b")]
                    bSL2 = [Buf("SL2a"), Buf("SL2b")]
                    bHID = [Buf("HID0"), Buf("HID1")]
                    bRL = [Buf(f"RL{j}") for j in range(4)]
                    bP = [Buf(f"P{i}") for i in range(8)]
                    bHrow = [Buf(f"H{t}") for t in range(NT)]
                    bHB2 = [Buf(f"HB2_{t}") for t in range(NT)]
                    bHBo = [Buf(f"HBo_{t}") for t in range(NT)]

                    def layer_norm_tiles(gcol, bcol, dst, bdst):
                        for w0 in range(0, NT, 3):
                            tl = list(range(w0, w0 + 3))
                            for i, t in enumerate(tl):
                                act([bHrow[t]], [bSBF[i]], SBF[i][:], H32[:, :, t * TT:(t + 1) * TT], AF.Copy)
                            for i, t in enumerate(tl):
                                pm = 2 * i
                                mm_group([bSBF[i]], [bP[pm]], [(PS[pm][:, :TT], ON1024, SBF[i][:, k, :], k == 0, k == 7)
                                                               for k in range(8)])
                            for k in range(8):
                                for i, t in enumerate(tl):
                                    t0 = t * TT
                                    dve_tt([bHrow[t], bP[2 * i]], [bHrow[t]], H32[:, k, t0:t0 + TT], H32[:, k, t0:t0 + TT],
                                           PS[2 * i][:, :TT], ALU.subtract)
                            for i, t in enumerate(tl):
                                act([bHrow[t]], [bSBF[i]], SBF[i][:], H32[:, :, t * TT:(t + 1) * TT], AF.Square)
                            for i, t in enumerate(tl):
                                pv_ = 2 * i + 1
                                mm_group([bSBF[i]], [bP[pv_]], [(PS[pv_][:, :TT], ON1024, SBF[i][:, k, :], k == 0, k == 7)
                                                                for k in range(8)])
                            for i, t in enumerate(tl):
                                pv_ = 2 * i + 1
                                rstd_from([bP[pv_]], bP[pv_], PS[pv_][:, :TT], PS[pv_][:, :TT], LN_EPS)
                            for k in range(8):
                                for i, t in enumerate(tl):
                                    t0 = t * TT
                                    dve_tt([bHrow[t], bP[2 * i + 1]], [bHrow[t]], H32[:, k, t0:t0 + TT],
                                           H32[:, k, t0:t0 + TT], PS[2 * i + 1][:, :TT], ALU.mult)
                                    act([bHrow[t]], [bHrow[t]], H32[:, k, t0:t0 + TT], H32[:, k, t0:t0 + TT], AF.Identity,
                                        scale=V_(gcol + k), bias=V_(bcol + k))
                            for i, t in enumerate(tl):
                                t0 = t * TT
                                dve_copy([bHrow[t]], [bdst[t]], dst[:, :, t0:t0 + TT], H32[:, :, t0:t0 + TT])

                    def load_w1(i, g):
                        c.dma(c.pool, bSL1[i], [], [bSL1[i]], out=SL1[i][:],
                              in_=w_ff1_d[l, :, 512 * g:512 * g + 512].rearrange("(k p) n -> p k n", p=128))

                    def load_w2(i, g):
                        c.dma(c.pool, bSL2[i], [], [bSL2[i]], out=SL2[i][:],
                              in_=w_ff2_d[l, 512 * g:512 * g + 512, :].rearrange("(j p) n -> p j n", p=128))

                    load_w1(0, 0)
                    load_w2(0, 0)
                    load_w1(1, 1)
                    load_w2(1, 1)

                    sc(f"LN1_s{s}_l{l}")
                    layer_norm_tiles(vb + VO_MIXG, vb + VO_MIXB, HB2, bHB2)

                    sc(f"FFN_s{s}_l{l}")
                    seq = [(g, t) for g in range(8) for t in range(NT)]

                    def ffn_w1(idx):
                        g, t = seq[idx]
                        i, t0, hi = g % 2, t * TT, idx % 2
                        for j in range(4):
                            mm_group([bHB2[t], bSL1[i]], [bP[j]],
                                     [(PS[j][:, :TT], SL1[i][:, k, 128 * j:128 * j + 128], HB2[:, k, t0:t0 + TT],
                                       k == 0, k == 7) for k in range(8)])
                            act([bP[j]], [bRL[j]], RL[j][:], PS[j][:, :TT], AF.Relu)
                            dve_tt([bRL[j], bP[j]], [bHID[hi]], HID[hi][:, j, :], PS[j][:, :TT], RL[j][:], ALU.mult)

                    def ffn_w2(idx):
                        g, t = seq[idx]
                        i, t0, hi = g % 2, t * TT, idx % 2
                        for m in range(8):
                            pf = 4 + m % 4
                            mm_group([bHID[hi], bSL2[i]], [bP[pf]],
                                     [(PS[pf][:, :TT], SL2[i][:, j, 128 * m:128 * m + 128], HID[hi][:, j, :],
                                       j == 0, j == 3) for j in range(4)])
                            dve_stt([bP[pf], bHrow[t]], [bHrow[t]], H32[:, m, t0:t0 + TT], H32[:, m, t0:t0 + TT],
                                    (ALPHA if g == 0 else 1.0), PS[pf][:, :TT], ALU.mult, ALU.add)

                    ffn_w1(0)
                    for idx in range(len(seq)):
                        if idx + 1 < len(seq):
                            ffn_w1(idx + 1)
                        ffn_w2(idx)
                        g, t = seq[idx]
                        if t == NT - 1 and g + 2 < 8:
                            load_w1(g % 2, g + 2)
                            load_w2(g % 2, g + 2)
                    sc(f"LN2_s{s}_l{l}")
                    layer_norm_tiles(vb + VO_FFG, vb + VO_FFB, HB, bHBo)
                    c.barrier()

            sc(f"out_s{s}")
            with ExitStack() as st:
                OUTT = [st.enter_context(nc.sbuf_tensor(nm("OUTT"), [128, D], F32)) for _ in range(2)]
                bO = [Buf("OUT_0"), Buf("OUT_1")]
                bP = [Buf(f"P{i}") for i in range(8)]
                bH = Buf("H32all")
                for blk in range(1, NB):
                    k0 = BO[blk]
                    i = blk % 2
                    for half in range(2):
                        pb = 2 * i + half
                        def emit_t(half=half, pb=pb):
                            last = None
                            for j in range(4):
                                ch = half * 4 + j
                                last = nc.tensor.transpose(PS[pb][:, j * 128:(j + 1) * 128], H32[:, ch, k0:k0 + 128], IDENT)
                            return last
                        c.op(c.pe, [bH], [bP[pb]], emit_t)
                        if half == 0:
                            act([bP[pb]], [bO[i]], OUTT[i][:, 0:512], PS[pb][:, :], AF.Copy)
                        else:
                            dve_copy([bP[pb]], [bO[i]], OUTT[i][:, 512:1024], PS[pb][:, :])
                    c.dma(c.sp, bO[i], [bO[i]], [], out=out_d[s, (blk - 1) * 128: blk * 128, :], in_=OUTT[i][:])
                c.barrier()
        sc(None)
    return nc


def _consts():
    ident = np.eye(128, dtype=np.float32)
    j = np.arange(128)[:, None]
    s_ = np.arange(128)[None, :]
    mask = (s_ > j).astype(np.float32)
    ui = (j >= s_).astype(np.float32)
    lx = (j < s_).astype(np.float32)
    o = np.ones((128, 128), np.float32)
    cf32 = np.concatenate([ident, mask], axis=1)
    cb16 = np.concatenate([ui, lx, o / 1024.0, o / 512.0, o / 256.0], axis=1).astype(ml_dtypes.bfloat16)
    return np.ascontiguousarray(cf32), np.ascontiguousarray(cb16)


def _vecs(inp):
    v = np.zeros((128, NV), np.float32)

    def col(a):
        return np.asarray(a, np.float32).reshape(-1, 128).T

    v[:, VC_LNIN_G:VC_LNIN_G + 8] = col(inp["ln_in_g"])
    v[:, VC_LNIN_B:VC_LNIN_B + 8] = col(inp["ln_in_b"])
    for l in range(DEPTH):
        b = VC_L0 + VC_PER * l
        v[:, b + VO_MIXG:b + VO_MIXG + 8] = col(inp["ln_mix_g"][l])
        v[:, b + VO_MIXB:b + VO_MIXB + 8] = col(inp["ln_mix_b"][l])
        v[:, b + VO_FFG:b + VO_FFG + 8] = col(inp["ln_ff_g"][l])
        v[:, b + VO_FFB:b + VO_FFB + 8] = col(inp["ln_ff_b"][l])
        v[:, b + VO_GMIX:b + VO_GMIX + 8] = col(inp["g_mix"][l])
        wc = np.asarray(inp["w_conf_dw"][l], np.float32)
        for cc in range(2):
            v[:, b + VO_CW + cc * 31: b + VO_CW + (cc + 1) * 31] = wc[:, cc * 128:(cc + 1) * 128].T
        v[:, b + VO_CB:b + VO_CB + 2] = col(inp["b_conf_dw"][l])
        v[:, b + VO_CG:b + VO_CG + 2] = col(inp["ln_conf_g"][l])
        v[:, b + VO_CBB:b + VO_CBB + 2] = col(inp["ln_conf_b"][l])
        ws = np.asarray(inp["w_short_dw"][l], np.float32)
        for cc in range(2):
            v[:, b + VO_SW + cc * 3: b + VO_SW + (cc + 1) * 3] = ws[:, cc * 128:(cc + 1) * 128].T
    return v


_NC_CACHE = {}


def run(inputs, nl=DEPTH, nseq=NSEQ, ncores=NCORES, trace=False):
    key = (nl, nseq)
    if key not in _NC_CACHE:
        _NC_CACHE[key] = build(nl, nseq)
    nc = _NC_CACHE[key]
    cf32, cb16 = _consts()
    vecs = _vecs(inputs)
    x = np.asarray(inputs["x"], np.float32)
    shared = {
        "meta": np.ascontiguousarray(inputs["meta_tokens"], np.float32),
        "w_in": np.ascontiguousarray(inputs["w_in"], np.float32),
        "w_out": np.ascontiguousarray(inputs["w_out"], np.float32),
        "w_ff1": np.ascontiguousarray(inputs["w_ff1"], np.float32),
        "w_ff2": np.ascontiguousarray(inputs["w_ff2"], np.float32),
        "vecs": vecs, "cf32": cf32, "cb16": cb16,
    }
    in_maps = []
    for i in range(ncores):
        m = dict(shared)
        m["x"] = np.ascontiguousarray(x[i * nseq:(i + 1) * nseq])
        in_maps.append(m)
    res = run_bass_kernel_spmd(nc, in_maps, core_ids=list(range(ncores)), **({"trace": True} if trace else {}))
    out = np.concatenate([r["out"] for r in res.results], axis=0)
    return out, res


def kernel(**inputs):
    out, _ = run(inputs)
    return out.astype(np.float32)
```
